# Optimizing a Trainium2 kernel written in Bass

```python
import math
import jax, jax.numpy as jnp
from jax import lax
import numpy as np

D_MODEL = 2048
BATCH = 8
SEQ = 2048
DEPTH = 2

LRU_WIDTH = 1024
LRU_BLOCKS = 8
LRU_BLOCK = LRU_WIDTH // LRU_BLOCKS
LRU_CONV = 4
LRU_C = 8.0
SC_WIDTH = 1024
SC_GROUPS = 8
SC_CONV = 3
N_HEADS = 16
QK_NOPE = 128
QK_ROPE = 64
V_HEAD = 128
KV_RANK = 512
ROPE_THETA = 10000.0
Q_BLOCK = 128
N_BRANCH = 3
D_FF = 5632
N_MOD = 9
EPS = 1e-6
IN_WIDTH = 2 * LRU_WIDTH + 3 * SC_WIDTH + N_HEADS * (QK_NOPE + QK_ROPE) + KV_RANK + QK_ROPE + N_BRANCH * D_MODEL

kernel_name = "hybrid_rglru_shortconv_mla_macaron_adaln"


def _in_sizes():
    return (LRU_WIDTH, LRU_WIDTH, SC_WIDTH, SC_WIDTH, SC_WIDTH,
            N_HEADS * (QK_NOPE + QK_ROPE), KV_RANK, QK_ROPE, N_BRANCH * D_MODEL)


def _split_last(x, sizes):
    idx, acc = [], 0
    for s in sizes[:-1]:
        acc += s
        idx.append(acc)
    return jnp.split(x, idx, axis=-1)


def rmsnorm(x, g):
    xf = x.astype(jnp.float32)
    y = xf * lax.rsqrt(jnp.mean(xf * xf, axis=-1, keepdims=True) + EPS)
    return (y * g.astype(jnp.float32)).astype(x.dtype)


def modulate(h, shift, scale):
    return h * (1 + scale[:, None, :]) + shift[:, None, :]


def causal_depthwise_conv(x, w):
    K, C = w.shape
    return lax.conv_general_dilated(
        x, w[:, None, :].astype(x.dtype), window_strides=(1,), padding=[(K - 1, 0)],
        dimension_numbers=('NWC', 'WIO', 'NWC'), feature_group_count=C)


def swiglu(h, w13, w2):
    g, u = jnp.split(h @ w13, 2, axis=-1)
    return (jax.nn.silu(g) * u) @ w2


def rope_angles(positions):
    half = QK_ROPE // 2
    inv = ROPE_THETA ** (-jnp.arange(half, dtype=jnp.float32) / half)
    ang = positions.astype(jnp.float32)[..., None] * inv
    return jnp.cos(ang), jnp.sin(ang)


def apply_rope(x, cos, sin):
    x1, x2 = jnp.split(x.astype(jnp.float32), 2, axis=-1)
    return jnp.concatenate([x1 * cos - x2 * sin, x2 * cos + x1 * sin], axis=-1).astype(x.dtype)


def rg_lru(x, wa, ba, wx, bx, lam):
    B, S, _ = x.shape
    xb = x.reshape(B, S, LRU_BLOCKS, LRU_BLOCK)
    r = jax.nn.sigmoid(jnp.einsum('bsgi,gij->bsgj', xb, wa).reshape(B, S, LRU_WIDTH) + ba)
    i = jax.nn.sigmoid(jnp.einsum('bsgi,gij->bsgj', xb, wx).reshape(B, S, LRU_WIDTH) + bx)
    log_a = (-LRU_C * jax.nn.softplus(-lam.astype(jnp.float32))) * r.astype(jnp.float32)
    a = jnp.exp(log_a)
    b = jnp.sqrt(-jnp.expm1(2.0 * log_a)) * (i * x).astype(jnp.float32)

    def combine(left, right):
        a1, b1 = left
        a2, b2 = right
        return a1 * a2, a2 * b1 + b2

    _, h = lax.associative_scan(combine, (a, b), axis=1)
    return h.astype(x.dtype)


def mla_attention(q, kv_lat, k_pe, positions, kv_norm_g, w_ukv):
    B, S, _ = q.shape
    q = q.reshape(B, S, N_HEADS, QK_NOPE + QK_ROPE)
    q_nope, q_pe = q[..., :QK_NOPE], q[..., QK_NOPE:]
    cos, sin = rope_angles(positions)
    q_pe = apply_rope(q_pe, cos[:, :, None, :], sin[:, :, None, :])
    k_pe = apply_rope(k_pe, cos, sin)
    kv = (rmsnorm(kv_lat, kv_norm_g) @ w_ukv).reshape(B, S, N_HEADS, QK_NOPE + V_HEAD)
    k_nope, v = kv[..., :QK_NOPE], kv[..., QK_NOPE:]
    scale = (QK_NOPE + QK_ROPE) ** -0.5
    outs = []
    for blk in range(S // Q_BLOCK):
        q0 = blk * Q_BLOCK
        kend = q0 + Q_BLOCK
        s = (jnp.einsum('bqhd,bkhd->bhqk', q_nope[:, q0:kend], k_nope[:, :kend])
             + jnp.einsum('bqhr,bkr->bhqk', q_pe[:, q0:kend], k_pe[:, :kend]))
        s = s.astype(jnp.float32) * scale
        mask = jnp.arange(kend)[None, :] <= (q0 + jnp.arange(Q_BLOCK))[:, None]
        s = jnp.where(mask, s, -jnp.inf)
        p = jax.nn.softmax(s, axis=-1).astype(v.dtype)
        outs.append(jnp.einsum('bhqk,bkhd->bqhd', p, v[:, :kend]))
    return jnp.concatenate(outs, axis=1).reshape(B, S, N_HEADS * V_HEAD)


def hybrid_layer(x, mod, positions, norm_g, ffn_w13, ffn_w2, w_in, lru_conv_w, lru_conv_b,
                 lru_wa, lru_ba, lru_wx, lru_bx, lru_lambda, lru_out, sc_conv_w, sc_out,
                 mla_kv_norm_g, mla_w_ukv, mla_out, w_o):
    sh1, sc1, g1, shm, scm, gm, sh2, sc2, g2 = jnp.split(mod, N_MOD, axis=-1)
    B, S, _ = x.shape

    h = modulate(rmsnorm(x, norm_g[0]), sh1, sc1)
    x = x + 0.5 * g1[:, None, :] * swiglu(h, ffn_w13[0], ffn_w2[0])

    h = modulate(rmsnorm(x, norm_g[1]), shm, scm)
    (lru_x, lru_gate, sc_b, sc_c, sc_x, q, kv_lat, k_pe, gate_logits) = _split_last(h @ w_in, _in_sizes())

    xr = causal_depthwise_conv(lru_x, lru_conv_w) + lru_conv_b
    y_lru = (rg_lru(xr, lru_wa, lru_ba, lru_wx, lru_bx, lru_lambda) * jax.nn.gelu(lru_gate)) @ lru_out

    y_sc = (sc_b * causal_depthwise_conv(sc_c * sc_x, sc_conv_w)) @ sc_out

    y_mla = mla_attention(q, kv_lat, k_pe, positions, mla_kv_norm_g, mla_w_ukv) @ mla_out

    gates = jax.nn.sigmoid(gate_logits.reshape(B, S, N_BRANCH, D_MODEL))
    merged = gates[:, :, 0] * y_lru + gates[:, :, 1] * y_sc + gates[:, :, 2] * y_mla
    x = x + gm[:, None, :] * (merged @ w_o)

    h = modulate(rmsnorm(x, norm_g[2]), sh2, sc2)
    x = x + 0.5 * g2[:, None, :] * swiglu(h, ffn_w13[1], ffn_w2[1])
    return x


def _dense(key, shape, fan_in, gain=1.0):
    return jax.random.normal(key, shape, jnp.float32) * (gain * fan_in ** -0.5)


def setup_inputs(seed: int = 0) -> dict:
    key = jax.random.key(seed)
    ks = jax.random.split(key, 24)
    L = DEPTH
    x = jax.random.normal(ks[0], (BATCH, SEQ, D_MODEL), jnp.float32)
    c = jax.random.normal(ks[1], (BATCH, D_MODEL), jnp.float32)
    positions = jnp.broadcast_to(jnp.arange(SEQ, dtype=jnp.int32)[None, :], (BATCH, SEQ))
    ada_w = _dense(ks[2], (L, D_MODEL, N_MOD * D_MODEL), D_MODEL, 0.5)
    ada_b = 0.02 * jax.random.normal(ks[3], (L, N_MOD * D_MODEL), jnp.float32)
    norm_g = 1.0 + 0.1 * jax.random.normal(ks[4], (L, 3, D_MODEL), jnp.float32)
    ffn_w13 = _dense(ks[5], (L, 2, D_MODEL, 2 * D_FF), D_MODEL)
    ffn_w2 = _dense(ks[6], (L, 2, D_FF, D_MODEL), D_FF)
    w_in = _dense(ks[7], (L, D_MODEL, IN_WIDTH), D_MODEL)
    lru_conv_w = _dense(ks[8], (L, LRU_CONV, LRU_WIDTH), LRU_CONV)
    lru_conv_b = 0.02 * jax.random.normal(ks[9], (L, LRU_WIDTH), jnp.float32)
    lru_wa = _dense(ks[10], (L, LRU_BLOCKS, LRU_BLOCK, LRU_BLOCK), LRU_BLOCK)
    lru_ba = 0.02 * jax.random.normal(ks[11], (L, LRU_WIDTH), jnp.float32)
    lru_wx = _dense(ks[12], (L, LRU_BLOCKS, LRU_BLOCK, LRU_BLOCK), LRU_BLOCK)
    lru_bx = 0.02 * jax.random.normal(ks[13], (L, LRU_WIDTH), jnp.float32)
    u = jax.random.uniform(ks[14], (L, LRU_WIDTH), jnp.float32, 0.9, 0.999)
    a0 = u ** (1.0 / LRU_C)
    lru_lambda = jnp.log(a0) - jnp.log1p(-a0)
    lru_out = _dense(ks[15], (L, LRU_WIDTH, D_MODEL), LRU_WIDTH)
    sc_conv_w = _dense(ks[16], (L, SC_CONV, SC_WIDTH), SC_CONV)
    sc_out = _dense(ks[17], (L, SC_WIDTH, D_MODEL), SC_WIDTH)
    mla_kv_norm_g = 1.0 + 0.1 * jax.random.normal(ks[18], (L, KV_RANK), jnp.float32)
    mla_w_ukv = _dense(ks[19], (L, KV_RANK, N_HEADS * (QK_NOPE + V_HEAD)), KV_RANK)
    mla_out = _dense(ks[20], (L, N_HEADS * V_HEAD, D_MODEL), N_HEADS * V_HEAD)
    w_o = _dense(ks[21], (L, D_MODEL, D_MODEL), D_MODEL)
    final_norm_g = 1.0 + 0.1 * jax.random.normal(ks[22], (D_MODEL,), jnp.float32)
    return {"x": x, "c": c, "positions": positions, "ada_w": ada_w, "ada_b": ada_b,
            "norm_g": norm_g, "ffn_w13": ffn_w13, "ffn_w2": ffn_w2, "w_in": w_in,
            "lru_conv_w": lru_conv_w, "lru_conv_b": lru_conv_b, "lru_wa": lru_wa,
            "lru_ba": lru_ba, "lru_wx": lru_wx, "lru_bx": lru_bx, "lru_lambda": lru_lambda,
            "lru_out": lru_out, "sc_conv_w": sc_conv_w, "sc_out": sc_out,
            "mla_kv_norm_g": mla_kv_norm_g, "mla_w_ukv": mla_w_ukv, "mla_out": mla_out,
            "w_o": w_o, "final_norm_g": final_norm_g}


def reference(x, c, positions, ada_w, ada_b, norm_g, ffn_w13, ffn_w2, w_in, lru_conv_w,
              lru_conv_b, lru_wa, lru_ba, lru_wx, lru_bx, lru_lambda, lru_out, sc_conv_w,
              sc_out, mla_kv_norm_g, mla_w_ukv, mla_out, w_o, final_norm_g):
    c_act = jax.nn.silu(c)
    for l in range(DEPTH):
        mod = c_act @ ada_w[l] + ada_b[l]
        x = hybrid_layer(x, mod, positions, norm_g[l], ffn_w13[l], ffn_w2[l], w_in[l],
                         lru_conv_w[l], lru_conv_b[l], lru_wa[l], lru_ba[l], lru_wx[l],
                         lru_bx[l], lru_lambda[l], lru_out[l], sc_conv_w[l], sc_out[l],
                         mla_kv_norm_g[l], mla_w_ukv[l], mla_out[l], w_o[l])
    return rmsnorm(x, final_norm_g)
```

```python
import numpy as np
import concourse.bass as bass
import concourse.mybir as mybir
from concourse.bass_utils import run_bass_kernel_spmd

F32 = mybir.dt.float32
BF16 = mybir.dt.bfloat16
I32 = mybir.dt.int32
AF = mybir.ActivationFunctionType
ALU = mybir.AluOpType

D = 2048
S = 2048
T = 512
NT = S // T
DEPTH = 2
DFF = 5632
NFF = DFF // 128
EPS = 1e-6
NS = 7
NSCR = 10
SCALE = float((128 + 64) ** -0.5)
TWO_PI = 6.283185307179586
ENGS = ['pe', 'act', 'dve', 'pool', 'sp']
STRICT_WAR = True

OFF = {}
_o = 0


def _add(name, n):
    global _o
    OFF[name] = _o
    _o += n


_add('c', 16)
for _l in range(DEPTH):
    _add('ada_b%d' % _l, 144)
    _add('ng%d' % _l, 48)
    _add('cw%d' % _l, 32)
    _add('cb%d' % _l, 8)
    _add('ba%d' % _l, 8)
    _add('bx%d' % _l, 8)
    _add('lam%d' % _l, 8)
    _add('scw%d' % _l, 24)
    _add('kvg%d' % _l, 4)
_add('fg', 16)
_add('invf', 1)
_add('sgn', 1)
NPRM = _o


class Buf:
    __slots__ = ('name', 'w', 'r')

    def __init__(self, name=''):
        self.name = name
        self.w = None
        self.r = {}


class Res:
    __slots__ = ('t', 'b')

    def __init__(self, t, b):
        self.t = t
        self.b = b


class Prog:
    def __init__(self, nc, ndma=32):
        self.nc = nc
        self.q = {e: [] for e in ENGS}
        self.sem = {e: nc.alloc_semaphore(name="s_" + e) for e in ENGS}
        self.cnt = {e: 0 for e in ENGS}
        self.waited = {e: {} for e in ENGS}
        self.dsem = [nc.alloc_semaphore(name="d_%d" % i) for i in range(ndma)]
        self.dcnt = [0] * ndma
        self.dpool = {'pool': list(range(0, ndma // 2)), 'sp': list(range(ndma // 2, ndma))}
        self.drr = {'pool': 0, 'sp': 0}
        self.dry = False
        self.wseq = []
        self.wi = 0
        self.wissued = 0
        self.wconsumed = 0
        self.wdone = set()
        self.slots = [nc.alloc_sbuf_tensor("wslot%d" % i, [128, 2048], BF16) for i in range(NS)]
        self.slotB = [Buf("slot%d" % i) for i in range(NS)]
        self.banks = [Res(nc.alloc_psum_tensor("bank%d" % i, [128, 512], F32), Buf("bank%d" % i)) for i in range(8)]
        self.bfree = list(self.banks)
        self.scr = [Res(nc.alloc_sbuf_tensor("scr%d" % i, [128, 516], F32), Buf("scr%d" % i)) for i in range(NSCR)]
        self.sfree = list(self.scr)
        self.pts = [Res(nc.alloc_sbuf_tensor("pt%d" % i, [128, 512], BF16), Buf("pt%d" % i)) for i in range(3)]
        self.pfree = list(self.pts)

    def bget(self):
        return self.bfree.pop(0)

    def bput(self, r):
        self.bfree.append(r)

    def sget(self):
        return self.sfree.pop(0)

    def sput(self, r):
        self.sfree.append(r)

    def pget(self):
        return self.pfree.pop(0)

    def pput(self, r):
        self.pfree.append(r)

    def _handle(self, key):
        return self.sem[key] if isinstance(key, str) else self.dsem[key]

    def _need(self, eng, toks):
        wd = self.waited[eng]
        best = {}
        for (k, v) in toks:
            if k == eng and eng == 'pe':
                continue
            if wd.get(k, 0) < v and best.get(k, 0) < v:
                best[k] = v
        for k, v in best.items():
            wd[k] = v
        return list(best.items())

    def _deps(self, eng, reads, writes):
        toks = []
        for b in reads:
            if b.w is not None:
                toks.append(b.w)
        for b in writes:
            if b.w is not None:
                toks.append(b.w)
            for k, v in b.r.items():
                if k != eng or STRICT_WAR:
                    toks.append((k, v))
        return toks

    def _mark(self, tok, reads, writes):
        k, v = tok
        for b in reads:
            b.r[k] = v
        for b in writes:
            b.w = tok
            b.r = {}

    def op(self, eng, fn, reads=(), writes=()):
        if self.dry:
            return
        waits = self._need(eng, self._deps(eng, reads, writes))
        self.cnt[eng] += 1
        tok = (eng, self.cnt[eng])
        self.q[eng].append((waits, fn, eng))
        self._mark(tok, reads, writes)

    def dma(self, eng, out, in_, reads=(), writes=()):
        if self.dry:
            return None
        lst = self.dpool[eng]
        k = lst[self.drr[eng]]
        self.drr[eng] = (self.drr[eng] + 1) % len(lst)
        toks = self._deps(eng, reads, writes)
        if self.dcnt[k] > 0:
            toks.append((k, self.dcnt[k]))
        waits = self._need(eng, toks)
        self.dcnt[k] += 16
        tok = (k, self.dcnt[k])
        self.q[eng].append((waits, lambda e: e.dma_start(out=out, in_=in_), k))
        self._mark(tok, reads, writes)
        return tok

    def wait_tokens(self, eng, toks):
        waits = self._need(eng, toks)
        if waits:
            self.q[eng].append((waits, None, None))

    def wnext(self, ap, n):
        if self.dry:
            self.wseq.append((ap, n))
            return (self.slots[0], self.slotB[0], -1)
        i = self.wi
        self.wi += 1
        assert i < self.wissued, "weight stream underrun %d %d" % (i, self.wissued)
        assert self.wseq[i][1] == n
        s = i % NS
        return (self.slots[s], self.slotB[s], i)

    def wfree(self, w):
        if self.dry:
            return
        self.wdone.add(w[2])
        while self.wconsumed in self.wdone:
            self.wdone.discard(self.wconsumed)
            self.wconsumed += 1
        self.wpump()

    def wpump(self):
        limit = min(len(self.wseq), self.wconsumed + NS)
        while self.wissued < limit:
            j = self.wissued
            s = j % NS
            ap, n = self.wseq[j]
            self.dma('pool', self.slots[s][:, 0:n], ap, writes=[self.slotB[s]])
            self.wissued += 1

    def emit(self):
        nc = self.nc
        prog = self

        def run(engname, e):
            for waits, fn, inc in prog.q[engname]:
                for k, v in waits:
                    e.wait_ge(prog._handle(k), v)
                if fn is None:
                    continue
                ins = fn(e)
                if isinstance(inc, str):
                    ins.then_inc(prog.sem[inc], 1)
                else:
                    ins.then_inc(prog.dsem[inc], 16)

        with nc.Block() as block:
            @block.tensor
            def _(e):
                run('pe', e)

            @block.scalar
            def _(e):
                run('act', e)

            @block.vector
            def _(e):
                run('dve', e)

            @block.gpsimd
            def _(e):
                run('pool', e)

            @block.sync
            def _(e):
                run('sp', e)


class Gen:
    def __init__(self, nc, nt_run=NT, n_layers=DEPTH, stop=None, dbg=False):
        self.nc = nc
        self.nt_run = nt_run
        self.n_layers = n_layers
        self.stop = stop
        self.dbg = dbg
        dt = nc.dram_tensor
        self.xT = dt("xT", [D, S], F32, kind="ExternalInput")
        self.pos = dt("pos", [128, S], I32, kind="ExternalInput")
        self.prm = dt("prm", [128, NPRM], F32, kind="ExternalInput")
        self.cst = dt("cst", [128, 256], F32, kind="ExternalInput")
        self.W = []
        for l in range(DEPTH):
            w = {}
            w['ada'] = dt("ada%d" % l, [144, 128, 2048], F32, kind="ExternalInput")
            for f in range(2):
                w['w13_%d' % f] = dt("w13_%d_%d" % (l, f), [88, 128, 2048], F32, kind="ExternalInput")
                w['w2_%d' % f] = dt("w2_%d_%d" % (l, f), [16, 128, NFF * 128], F32, kind="ExternalInput")
            w['win'] = dt("win%d" % l, [117, 128, 2048], F32, kind="ExternalInput")
            w['lruw'] = dt("lruw%d" % l, [8, 128, 256], F32, kind="ExternalInput")
            w['lruout'] = dt("lruout%d" % l, [8, 128, 2048], F32, kind="ExternalInput")
            w['scout'] = dt("scout%d" % l, [8, 128, 2048], F32, kind="ExternalInput")
            w['ukvk'] = dt("ukvk%d" % l, [4, 128, 2048], F32, kind="ExternalInput")
            w['ukvv'] = dt("ukvv%d" % l, [4, 128, 2048], F32, kind="ExternalInput")
            w['mlaout'] = dt("mlaout%d" % l, [16, 128, 2048], F32, kind="ExternalInput")
            w['wo'] = dt("wo%d" % l, [16, 128, 2048], F32, kind="ExternalInput")
            self.W.append(w)
        self.outT = dt("outT", [D, S], F32, kind="ExternalOutput")
        if dbg:
            self.dbgact = dt("dbgact", [128, 48 * T], BF16, kind="ExternalOutput")

        P = self.P = Prog(nc)
        sb = nc.alloc_sbuf_tensor
        self.xt = sb("xt", [128, 16, T], F32)
        self.xB = [Buf("x%d" % c) for c in range(16)]
        self.hT = sb("hT", [128, 16, T], BF16)
        self.hB = [Buf("h%d" % c) for c in range(16)]
        self.act = sb("act", [128, 48, T], BF16)
        self.actB = [Buf("act%d" % c) for c in range(48)]
        self.ckv = [sb("ckv%d" % l, [128, 4, S], BF16) for l in range(DEPTH)]
        self.ckvB = [Buf("ckv%d" % l) for l in range(DEPTH)]
        self.kpe = [sb("kpe%d" % l, [128, S], BF16) for l in range(DEPTH)]
        self.kpeB = [Buf("kpe%d" % l) for l in range(DEPTH)]
        self.kn = sb("kn", [128, S], BF16)
        self.knB = [Buf("kn%d" % i) for i in range(NT)]
        self.vg = sb("vg", [128, S], BF16)
        self.vgB = [Buf("vg%d" % i) for i in range(NT)]
        self.prm_sb = sb("prm_sb", [128, NPRM], F32)
        self.prmB = Buf("prm")
        self.mod = [sb("mod%d" % l, [128, 144], F32) for l in range(DEPTH)]
        self.vec = [sb("vec%d" % l, [128, 96], F32) for l in range(DEPTH)]
        self.vecB = Buf("vec")
        self.modB = [[Buf("mod%d_%d" % (l, j)) for j in range(9)] for l in range(DEPTH)]
        self.lamB = Buf("lam")
        self.dummy = sb("dummy_ln", [128, 2], F32)
        self.dummyB = Buf("dummy")
        self.need_rope = None
        self.cur_layer = 0
        self.cactB = Buf("cact")
        self.ada_pending = []
        self.ada_bank = None
        self.halo = [sb("halo%d" % l, [128, 24], F32) for l in range(DEPTH)]
        self.shalo = [sb("shalo%d" % l, [128, 16], F32) for l in range(DEPTH)]
        self.hst = [sb("hst%d" % l, [128, 8], F32) for l in range(DEPTH)]
        self.haloB = [Buf() for l in range(DEPTH)]
        self.shaloB = [Buf() for l in range(DEPTH)]
        self.hstB = [Buf() for l in range(DEPTH)]
        self.ones32 = sb("ones32", [128, 128], F32)
        self.onesbf = sb("onesbf", [128, 128], BF16)
        self.mask = sb("mask", [128, 128], BF16)
        self.perm = sb("perm", [128, 128], F32)
        self.cact = sb("cact", [128, 16], BF16)
        self.cstB = Buf("cst")
        self.posi = sb("posi", [128, T], I32)
        self.posiB = Buf("posi")
        self.cos_t = sb("cos_t", [128, T], F32)
        self.sin_t = sb("sin_t", [128, T], F32)
        self.ropeB = Buf("rope")
        self.out_toks = []

    def prm_ap(self, name, a=0, b=None):
        o = OFF[name]
        if b is None:
            b = a + 1
        return self.prm_sb[:, o + a:o + b]

    def mmg(self, out_ap, pairs, reads, writes, start=True, stop=True):
        n = len(pairs)

        def fn(e):
            ins = None
            for i, (l, r) in enumerate(pairs):
                ins = e.matmul(out_ap, lhsT=l, rhs=r, start=(start and i == 0), stop=(stop and i == n - 1))
            return ins
        self.P.op('pe', fn, reads, writes)

    def ACT(self, out, in_, func, reads, writes, **kw):
        self.P.op('act', lambda e: e.activation(out=out, in_=in_, func=func, **kw), reads, writes)

    def TT(self, out, a, b, op, reads, writes):
        self.P.op('dve', lambda e: e.tensor_tensor(out=out, in0=a, in1=b, op=op), reads, writes)

    def TS(self, out, a, s1, s2, op0, op1, reads, writes):
        self.P.op('dve', lambda e: e.tensor_scalar(out=out, in0=a, scalar1=s1, scalar2=s2, op0=op0, op1=op1), reads, writes)

    def STT(self, out, a, s, b, op0, op1, reads, writes):
        self.P.op('dve', lambda e: e.scalar_tensor_tensor(out=out, in0=a, scalar=s, in1=b, op0=op0, op1=op1), reads, writes)

    def CP(self, eng, out, in_, reads, writes):
        if eng == 'act':
            self.P.op('act', lambda e: e.activation(out=out, in_=in_, func=AF.Copy), reads, writes)
        else:
            self.P.op(eng, lambda e: e.tensor_copy(out=out, in_=in_), reads, writes)

    def proj16(self, wdram, idx):
        P = self.P
        w = P.wnext(wdram[idx, :, :], 2048)
        bk = P.bget()
        pairs = [(w[0][:, kc * 128:(kc + 1) * 128], self.hT[:, kc, :]) for kc in range(16)]
        self.mmg(bk.t[:, :], pairs, [w[1]] + self.hB, [bk.b])
        P.wfree(w)
        return bk

    def acc_begin(self):
        return [self.P.sget(), 0]

    def acc_add(self, st, ap, b):
        P = self.P
        acc = st[0]
        if st[1] == 0:
            self.ACT(acc.t[:, 0:T], ap, AF.Square, [b], [acc.b])
        else:
            sq = P.sget()
            self.ACT(sq.t[:, 0:T], ap, AF.Square, [b], [sq.b])
            self.TT(acc.t[:, 0:T], acc.t[:, 0:T], sq.t[:, 0:T], ALU.add, [acc.b, sq.b], [acc.b])
            P.sput(sq)
        st[1] += 1

    def acc_finish(self, st, inv_n):
        P = self.P
        acc = st[0]
        bank = P.bget()
        self.mmg(bank.t[:, :], [(self.ones32[:, :], acc.t[:, 0:T])], [acc.b, self.cstB], [bank.b])
        self.ACT(acc.t[:, 0:T], bank.t[:, :], AF.Ln, [bank.b], [acc.b], scale=inv_n, bias=EPS)
        P.bput(bank)
        self.ACT(acc.t[:, 0:T], acc.t[:, 0:T], AF.Exp, [acc.b], [acc.b], scale=-0.5)
        return acc

    def preload_ln(self):
        self.ACT(self.dummy[:, 0:1], self.dummy[:, 1:2], AF.Ln, [self.dummyB], [self.dummyB])

    def xacc_begin(self):
        self.xacc = self.acc_begin()

    def xacc_add(self, c):
        self.acc_add(self.xacc, self.xt[:, c, :], self.xB[c])

    def norm_mod(self, A, B, AB, BB):
        P = self.P
        rstd = self.acc_finish(self.xacc, 1.0 / D)
        for c in range(16):
            tmp = P.sget()
            self.STT(tmp.t[:, 0:T], self.xt[:, c, :], A[:, c:c + 1], rstd.t[:, 0:T], ALU.mult, ALU.mult,
                     [self.xB[c], rstd.b, AB], [tmp.b])
            self.ACT(self.hT[:, c, :], tmp.t[:, 0:T], AF.Identity, [tmp.b, BB], [self.hB[c]],
                     bias=B[:, c:c + 1], scale=1.0)
            P.sput(tmp)
        P.sput(rstd)

    def proj_pair_kouter(self, wdA, iA, wdB, iB):
        P = self.P
        wA = P.wnext(wdA[iA, :, :], 2048)
        wB = P.wnext(wdB[iB, :, :], 2048)
        bA = P.bget()
        bB = P.bget()
        for kc in range(16):
            for (w, bk) in ((wA, bA), (wB, bB)):
                P.op('pe', lambda e, o=bk.t[:, :], l_=w[0][:, kc * 128:(kc + 1) * 128], r_=self.hT[:, kc, :],
                     st=(kc == 0), sp_=(kc == 15): e.matmul(o, lhsT=l_, rhs=r_, start=st, stop=sp_),
                     [w[1], self.hB[kc]], [bk.b])
        P.wfree(wA)
        P.wfree(wB)
        return bA, bB

    def ada_group(self, l, j):
        P = self.P
        bank = P.bget()
        for jj in range(16):
            w = P.wnext(self.W[l]['ada'][16 * j + jj, :, :], 2048)
            pairs = [(w[0][:, kc * 128:(kc + 1) * 128], self.cact[:, kc:kc + 1]) for kc in range(16)]
            self.mmg(bank.t[:, jj:jj + 1], pairs, [w[1], self.cactB], [bank.b])
            P.wfree(w)
        mod = self.mod[l]
        vec = self.vec[l]
        mB = self.modB[l][j]
        self.TT(mod[:, 16 * j:16 * j + 16], bank.t[:, 0:16], self.prm_ap('ada_b%d' % l, 16 * j, 16 * j + 16), ALU.add,
                [bank.b, self.prmB], [mB])
        P.bput(bank)
        ng = 'ng%d' % l
        rw = [mB, self.prmB]
        if j == 1:
            self.STT(vec[:, 0:16], mod[:, 16:32], 1.0, self.prm_ap(ng, 0, 16), ALU.add, ALU.mult, rw, [mB])
        elif j == 2:
            self.TS(vec[:, 16:32], mod[:, 32:48], 0.5, 0.0, ALU.mult, ALU.add, rw, [mB])
        elif j == 4:
            self.STT(vec[:, 32:48], mod[:, 64:80], 1.0, self.prm_ap(ng, 16, 32), ALU.add, ALU.mult, rw, [mB])
        elif j == 7:
            self.STT(vec[:, 48:64], mod[:, 112:128], 1.0, self.prm_ap(ng, 32, 48), ALU.add, ALU.mult, rw, [mB])
        elif j == 8:
            self.TS(vec[:, 64:80], mod[:, 128:144], 0.5, 0.0, ALU.mult, ALU.add, rw, [mB])

    def ada_need(self, l, j):
        while (l, j) in self.ada_pending:
            self.ada_group(*self.ada_pending.pop(0))

    def ada_bg(self, k=1):
        for _ in range(k):
            if self.ada_pending:
                l, j = self.ada_pending[0]
                if l == self.cur_layer or (l == self.cur_layer + 1 and j <= 1):
                    self.ada_group(*self.ada_pending.pop(0))

    def prologue(self):
        P = self.P
        nc = self.nc
        P.dma('sp', self.prm_sb[:, :], self.prm[:, :], writes=[self.prmB])
        P.dma('pool', self.mask[:, :], self.cst[:, 0:128], writes=[self.cstB])
        P.dma('sp', self.perm[:, :], self.cst[:, 128:256], writes=[self.cstB])
        P.op('dve', lambda e: e.memset(self.ones32[:, :], 1.0), [], [self.cstB])
        P.op('dve', lambda e: e.memset(self.onesbf[:, :], 1.0), [], [self.cstB])
        P.op('dve', lambda e: e.memset(self.dummy[:, :], 1.0), [], [self.dummyB])
        for l in range(DEPTH):
            P.op('dve', lambda e, l=l: e.memset(self.halo[l][:, :], 0.0), [], [self.haloB[l]])
            P.op('dve', lambda e, l=l: e.memset(self.shalo[l][:, :], 0.0), [], [self.shaloB[l]])
            P.op('dve', lambda e, l=l: e.memset(self.hst[l][:, :], 0.0), [], [self.hstB[l]])
        self.ACT(self.cact[:, :], self.prm_ap('c', 0, 16), AF.Silu, [self.prmB], [self.cactB])
        for l in range(self.n_layers):
            vec = self.vec[l]
            rw = [self.lamB, self.prmB]
            self.ACT(vec[:, 80:88], self.prm_ap('lam%d' % l, 0, 8), AF.Exp, rw, [self.lamB], scale=-1.0)
            self.ACT(vec[:, 80:88], vec[:, 80:88], AF.Ln, rw, [self.lamB], bias=1.0)
            self.TS(vec[:, 80:88], vec[:, 80:88], -8.0, 0.0, ALU.mult, ALU.add, rw, [self.lamB])
            self.TS(vec[:, 88:96], vec[:, 80:88], 2.0, 0.0, ALU.mult, ALU.add, rw, [self.lamB])
        self.ada_pending = [(l, j) for l in range(self.n_layers) for j in range(9)]

    def rope_tables(self, t):
        P = self.P
        t0 = t * T
        P.dma('sp', self.posi[:, :], self.pos[:, t0:t0 + T], writes=[self.posiB])
        turns = P.sget()
        self.CP('dve', turns.t[:, 0:T], self.posi[:, :], [self.posiB], [turns.b])
        self.TS(turns.t[:, 0:T], turns.t[:, 0:T], self.prm_ap('invf'), 1.0 / TWO_PI, ALU.mult, ALU.mult,
                [turns.b, self.prmB], [turns.b])
        for which in range(2):
            tc = P.sget()
            self.TS(tc.t[:, 0:T], turns.t[:, 0:T], 0.25 if which == 1 else 0.0, 0.0, ALU.add, ALU.add,
                    [turns.b], [tc.b])
            self.CP('dve', self.posi[:, :], tc.t[:, 0:T], [tc.b], [self.posiB])
            rf = P.sget()
            self.CP('dve', rf.t[:, 0:T], self.posi[:, :], [self.posiB], [rf.b])
            self.TT(rf.t[:, 0:T], tc.t[:, 0:T], rf.t[:, 0:T], ALU.subtract, [tc.b, rf.b], [rf.b])
            dst = self.cos_t if which == 1 else self.sin_t
            self.ACT(dst[:, :], rf.t[:, 0:T], AF.Sin, [rf.b], [self.ropeB], scale=TWO_PI * (1.0 - 1e-6))
            P.sput(tc)
            P.sput(rf)
        P.sput(turns)
        self.TS(self.sin_t[:, :], self.sin_t[:, :], self.prm_ap('sgn'), 0.0, ALU.mult, ALU.add,
                [self.ropeB, self.prmB], [self.ropeB])

    def rope_apply(self, bA, dst, dstB):
        P = self.P
        sA = P.sget()
        self.CP('act', sA.t[:, 0:T], bA.t[:, :], [bA.b], [sA.b])
        P.bput(bA)
        bB = P.bget()
        self.mmg(bB.t[:, :], [(self.perm[:, :], sA.t[:, 0:T])], [self.cstB, sA.b], [bB.b])
        t1 = P.sget()
        t2 = P.sget()
        self.TT(t1.t[:, 0:T], sA.t[:, 0:T], self.cos_t[:, :], ALU.mult, [sA.b, self.ropeB], [t1.b])
        self.TT(t2.t[:, 0:T], bB.t[:, :], self.sin_t[:, :], ALU.mult, [bB.b, self.ropeB], [t2.b])
        P.sput(sA)
        P.bput(bB)
        self.TT(dst, t1.t[:, 0:T], t2.t[:, 0:T], ALU.add, [t1.b, t2.b], [dstB])
        P.sput(t1)
        P.sput(t2)

    def ffn(self, l, f):
        P = self.P
        vec = self.vec[l]
        mod = self.mod[l]
        mB = self.modB[l]
        if f == 0:
            jB, jA, jG = 0, 1, 2
            A, B, G = vec[:, 0:16], mod[:, 0:16], vec[:, 16:32]
        else:
            jB, jA, jG = 6, 7, 8
            A, B, G = vec[:, 48:64], mod[:, 96:112], vec[:, 64:80]
        self.ada_need(l, jB)
        self.ada_need(l, jA)
        self.norm_mod(A, B, mB[jA], mB[jB])
        w13 = self.W[l]['w13_%d' % f]
        w2 = self.W[l]['w2_%d' % f]
        for j in range(NFF):
            if j == 0:
                bg, bu = self.proj_pair_kouter(w13, 0, w13, 1)
            else:
                bg = self.proj16(w13, 2 * j)
                bu = self.proj16(w13, 2 * j + 1)
            sg = P.sget()
            self.ACT(sg.t[:, 0:T], bg.t[:, :], AF.Silu, [bg.b], [sg.b])
            P.bput(bg)
            self.TT(self.act[:, j, :], sg.t[:, 0:T], bu.t[:, :], ALU.mult, [sg.b, bu.b], [self.actB[j]])
            P.bput(bu)
            P.sput(sg)
            if j in (14, 29):
                self.ada_bg(1)
            if j == 3 and self.need_rope is not None:
                self.rope_tables(self.need_rope)
                self.need_rope = None
        self.preload_ln()
        self.ada_need(l, jG)
        parts = [(0, 16), (16, 32), (32, NFF)]
        self.xacc_begin()
        for n in range(16):
            units = [P.wnext(w2[n, :, k0 * 128:k1 * 128], (k1 - k0) * 128) for (k0, k1) in parts]
            by = P.bget()
            pairs = []
            for (k0, k1), w in zip(parts, units):
                for kc in range(k0, k1):
                    pairs.append((w[0][:, (kc - k0) * 128:(kc - k0 + 1) * 128], self.act[:, kc, :]))
            self.mmg(by.t[:, :], pairs, [w[1] for w in units] + self.actB[0:NFF], [by.b])
            for w in units:
                P.wfree(w)
            self.STT(self.xt[:, n, :], by.t[:, :], G[:, n:n + 1], self.xt[:, n, :], ALU.mult, ALU.add,
                     [by.b, self.xB[n], mB[jG]], [self.xB[n]])
            P.bput(by)
            self.xacc_add(n)
            if n == 7:
                self.ada_bg(1)

    def attn_K(self, l, t, h, uk):
        P = self.P
        hh = h % 4
        ckv = self.ckv[l]
        for s in range(t + 1):
            bk = P.bget()
            pairs = [(uk[0][:, (hh * 4 + kc) * 128:(hh * 4 + kc + 1) * 128], ckv[:, kc, s * T:(s + 1) * T])
                     for kc in range(4)]
            self.mmg(bk.t[:, :], pairs, [uk[1], self.ckvB[l]], [bk.b])
            self.CP('act', self.kn[:, s * T:(s + 1) * T], bk.t[:, :], [bk.b], [self.knB[s]])
            P.bput(bk)

    def attn_V(self, l, t, h, uv):
        P = self.P
        hh = h % 4
        ckv = self.ckv[l]
        for s in range(t + 1):
            bv = P.bget()
            for i in range(4):
                kb = s * 4 + i
                pairs = [(ckv[:, kc, kb * 128:(kb + 1) * 128],
                          uv[0][:, (kc * 4 + hh) * 128:(kc * 4 + hh + 1) * 128]) for kc in range(4)]
                self.mmg(bv.t[:, i * 128:(i + 1) * 128], pairs, [uv[1], self.ckvB[l]], [bv.b])
            self.CP('dve', self.vg[:, s * T:(s + 1) * T], bv.t[:, :], [bv.b], [self.vgB[s]])
            P.bput(bv)

    def mixer(self, l, t):
        P = self.P
        W = self.W[l]
        win = W['win']
        vec = self.vec[l]
        mod = self.mod[l]
        mB = self.modB[l]
        t0 = t * T
        act = self.act
        actB = self.actB
        self.ada_need(l, 3)
        self.ada_need(l, 4)
        self.norm_mod(vec[:, 32:48], mod[:, 48:64], mB[4], mB[3])
        rwv = [self.lamB, self.prmB]
        cw = 'cw%d' % l
        scw = 'scw%d' % l
        def sc_g(g):
            bc = self.proj16(win, 16 + 3 * g)
            cs = P.sget()
            self.CP('act', cs.t[:, 0:T], bc.t[:, :], [bc.b], [cs.b])
            P.bput(bc)
            bxx = self.proj16(win, 16 + 3 * g + 1)
            sxw = P.sget()
            self.CP('act', sxw.t[:, 0:2], self.shalo[l][:, g * 2:g * 2 + 2], [self.shaloB[l]], [sxw.b])
            self.TT(sxw.t[:, 2:2 + T], cs.t[:, 0:T], bxx.t[:, :], ALU.mult, [cs.b, bxx.b], [sxw.b])
            P.bput(bxx)
            P.sput(cs)
            self.CP('act', self.shalo[l][:, g * 2:g * 2 + 2], sxw.t[:, T:T + 2], [sxw.b], [self.shaloB[l]])
            v = P.sget()
            self.TS(v.t[:, 0:T], sxw.t[:, 2:2 + T], self.prm_ap(scw, g * 3 + 2), 0.0, ALU.mult, ALU.add,
                    [sxw.b] + rwv, [v.b])
            for k in range(2):
                self.STT(v.t[:, 0:T], sxw.t[:, k:k + T], self.prm_ap(scw, g * 3 + k), v.t[:, 0:T],
                         ALU.mult, ALU.add, [sxw.b, v.b] + rwv, [v.b])
            P.sput(sxw)
            bb = self.proj16(win, 16 + 3 * g + 2)
            self.TT(act[:, 32 + g, :], v.t[:, 0:T], bb.t[:, :], ALU.mult, [v.b, bb.b], [actB[32 + g]])
            P.bput(bb)
            P.sput(v)
            if g == 7:
                self.ada_bg(1)
        for g in range(8):
            if g == 0:
                bx, bgt = self.proj_pair_kouter(win, 0, win, 1)
            else:
                bx = self.proj16(win, 2 * g)
                bgt = self.proj16(win, 2 * g + 1)
            lxw = P.sget()
            self.CP('act', lxw.t[:, 0:3], self.halo[l][:, g * 3:g * 3 + 3], [self.haloB[l]], [lxw.b])
            self.CP('act', lxw.t[:, 3:3 + T], bx.t[:, :], [bx.b], [lxw.b])
            P.bput(bx)
            self.CP('act', self.halo[l][:, g * 3:g * 3 + 3], lxw.t[:, T:T + 3], [lxw.b], [self.haloB[l]])
            gx = P.sget()
            self.CP('act', gx.t[:, 0:T], bgt.t[:, :], [bgt.b], [gx.b])
            P.bput(bgt)
            xr = P.sget()
            self.TS(xr.t[:, 0:T], lxw.t[:, 3:3 + T], self.prm_ap(cw, g * 4 + 3), self.prm_ap('cb%d' % l, g),
                    ALU.mult, ALU.add, [lxw.b] + rwv, [xr.b])
            for k in range(3):
                self.STT(xr.t[:, 0:T], lxw.t[:, k:k + T], self.prm_ap(cw, g * 4 + k), xr.t[:, 0:T],
                         ALU.mult, ALU.add, [lxw.b, xr.b] + rwv, [xr.b])
            P.sput(lxw)
            xrb = P.pget()
            self.CP('act', xrb.t[:, :], xr.t[:, 0:T], [xr.b], [xrb.b])
            sc_g(g)
            lw = P.wnext(W['lruw'][g, :, :], 256)
            br = P.bget()
            self.mmg(br.t[:, :], [(lw[0][:, 0:128], xrb.t[:, :])], [lw[1], xrb.b], [br.b])
            bi = P.bget()
            self.mmg(bi.t[:, :], [(lw[0][:, 128:256], xrb.t[:, :])], [lw[1], xrb.b], [bi.b])
            P.wfree(lw)
            P.pput(xrb)
            u = P.sget()
            self.TT(u.t[:, 0:T], gx.t[:, 0:T], gx.t[:, 0:T], ALU.mult, [gx.b], [u.b])
            self.TS(u.t[:, 0:T], u.t[:, 0:T], 0.044715, 1.0, ALU.mult, ALU.add, [u.b], [u.b])
            self.TT(u.t[:, 0:T], u.t[:, 0:T], gx.t[:, 0:T], ALU.mult, [u.b, gx.b], [u.b])
            self.ACT(u.t[:, 0:T], u.t[:, 0:T], AF.Sigmoid, [u.b], [u.b], scale=1.5957691216057308)
            self.TT(u.t[:, 0:T], u.t[:, 0:T], gx.t[:, 0:T], ALU.mult, [u.b, gx.b], [u.b])
            P.sput(gx)
            r = P.sget()
            self.ACT(r.t[:, 0:T], br.t[:, :], AF.Sigmoid, [br.b] + rwv, [r.b], bias=self.prm_ap('ba%d' % l, g))
            P.bput(br)
            ii = P.sget()
            self.ACT(ii.t[:, 0:T], bi.t[:, :], AF.Sigmoid, [bi.b] + rwv, [ii.b], bias=self.prm_ap('bx%d' % l, g))
            P.bput(bi)
            self.TT(ii.t[:, 0:T], ii.t[:, 0:T], xr.t[:, 0:T], ALU.mult, [ii.b, xr.b], [ii.b])
            P.sput(xr)
            a2 = P.sget()
            self.ACT(a2.t[:, 0:T], r.t[:, 0:T], AF.Exp, [r.b] + rwv, [a2.b], scale=vec[:, 88 + g:89 + g])
            self.ACT(r.t[:, 0:T], r.t[:, 0:T], AF.Exp, [r.b] + rwv, [r.b], scale=vec[:, 80 + g:81 + g])
            self.ACT(a2.t[:, 0:T], a2.t[:, 0:T], AF.Sqrt, [a2.b], [a2.b], scale=-1.0, bias=1.0)
            self.TT(ii.t[:, 0:T], ii.t[:, 0:T], a2.t[:, 0:T], ALU.mult, [ii.b, a2.b], [ii.b])
            P.sput(a2)
            hb = P.sget()
            P.op('dve', lambda e, o=hb.t[:, 0:T], a=r.t[:, 0:T], b=ii.t[:, 0:T], init=self.hst[l][:, g:g + 1]:
                 e.tensor_tensor_scan(out=o, data0=a, data1=b, initial=init, op0=ALU.mult, op1=ALU.add),
                 [r.b, ii.b, self.hstB[l]], [hb.b])
            P.sput(r)
            P.sput(ii)
            self.CP('act', self.hst[l][:, g:g + 1], hb.t[:, T - 1:T], [hb.b], [self.hstB[l]])
            self.TT(act[:, 24 + g, :], u.t[:, 0:T], hb.t[:, 0:T], ALU.mult, [u.b, hb.b], [actB[24 + g]])
            P.sput(u)
            P.sput(hb)
            if g == 7:
                self.ada_bg(1)
        kvf = []
        kacc = self.acc_begin()
        for c in range(4):
            bk = self.proj16(win, 40 + c)
            s_ = P.sget()
            self.CP('act', s_.t[:, 0:T], bk.t[:, :], [bk.b], [s_.b])
            P.bput(bk)
            kvf.append(s_)
            self.acc_add(kacc, s_.t[:, 0:T], s_.b)
        bA = self.proj16(win, 44)
        self.rope_apply(bA, self.kpe[l][:, t0:t0 + T], self.kpeB[l])
        rstd = self.acc_finish(kacc, 1.0 / 512)
        for c in range(4):
            self.STT(self.ckv[l][:, c, t0:t0 + T], kvf[c].t[:, 0:T], self.prm_ap('kvg%d' % l, c), rstd.t[:, 0:T],
                     ALU.mult, ALU.mult, [kvf[c].b, rstd.b] + rwv, [self.ckvB[l]])
            P.sput(kvf[c])
        P.sput(rstd)
        for hp in range(8):
            for e_ in range(2):
                h = 2 * hp + e_
                bq = self.proj16(win, 45 + 3 * hp + e_)
                self.CP('act', act[:, h, :], bq.t[:, :], [bq.b], [actB[h]])
                P.bput(bq)
            bA = self.proj16(win, 45 + 3 * hp + 2)
            self.rope_apply(bA, act[:, 16 + hp, :], actB[16 + hp])
            if hp in (3, 7):
                self.ada_bg(1)
        nkb = 4 * (t + 1)
        kpe = self.kpe[l]
        uk = P.wnext(W['ukvk'][0, :, :], 2048)
        self.attn_K(l, t, 0, uk)
        uv = None
        for h in range(16):
            G, hh = divmod(h, 4)
            hp, hb_ = divmod(h, 2)
            base = 64 * hb_
            if hh == 0:
                uv = P.wnext(W['ukvv'][G, :, :], 2048)
            self.attn_V(l, t, h, uv)
            if hh == 3:
                P.wfree(uv)
            bo = P.bget()
            dacc = P.sget()

            def emit_S(kb):
                j = kb - 4 * t
                q0 = 0 if j < 0 else j * 128
                bs = P.bget()
                pairs = [(self.kn[:, kb * 128:(kb + 1) * 128], act[:, h, q0:T]),
                         (kpe[base:base + 64, kb * 128:(kb + 1) * 128], act[base:base + 64, 16 + hp, q0:T])]
                self.mmg(bs.t[:, q0:T], pairs, [self.knB[kb // 4], actB[h], self.kpeB[l], actB[16 + hp]], [bs.b])
                pt = P.pget()
                self.ACT(pt.t[:, q0:T], bs.t[:, q0:T], AF.Exp, [bs.b], [pt.b], scale=SCALE)
                P.bput(bs)
                if j >= 0:
                    self.TT(pt.t[:, q0:q0 + 128], pt.t[:, q0:q0 + 128], self.mask[:, :], ALU.mult,
                            [pt.b, self.cstB], [pt.b])
                return (kb, q0, pt)

            def emit_PV(kb, q0, pt):
                self.mmg(bo.t[:, q0:T], [(self.vg[:, kb * 128:(kb + 1) * 128], pt.t[:, q0:T])],
                         [self.vgB[kb // 4], pt.b], [bo.b], start=(kb == 0), stop=(kb == nkb - 1))
                if kb == 0:
                    self.CP('dve', dacc.t[:, 0:T], pt.t[:, :], [pt.b], [dacc.b])
                else:
                    self.TT(dacc.t[:, q0:T], dacc.t[:, q0:T], pt.t[:, q0:T], ALU.add, [dacc.b, pt.b], [dacc.b])
                P.pput(pt)

            pend = []
            for kb in range(nkb):
                pend.append(emit_S(kb))
                if len(pend) > 2:
                    emit_PV(*pend.pop(0))
            while len(pend) > 1:
                emit_PV(*pend.pop(0))
            pend = pend[0]
            if h < 15:
                if (h + 1) % 4 == 0:
                    P.wfree(uk)
                    uk = P.wnext(W['ukvk'][(h + 1) // 4, :, :], 2048)
                self.attn_K(l, t, h + 1, uk)
            else:
                P.wfree(uk)
            emit_PV(*pend)
            bd = P.bget()
            self.mmg(bd.t[:, :], [(self.ones32[:, :], dacc.t[:, 0:T])], [dacc.b, self.cstB], [bd.b])
            self.ACT(dacc.t[:, 0:T], bd.t[:, :], AF.Ln, [bd.b], [dacc.b])
            P.bput(bd)
            self.ACT(dacc.t[:, 0:T], dacc.t[:, 0:T], AF.Exp, [dacc.b], [dacc.b], scale=-1.0)
            self.TT(act[:, h, :], bo.t[:, :], dacc.t[:, 0:T], ALU.mult, [bo.b, dacc.b], [actB[h]])
            P.bput(bo)
            P.sput(dacc)
        if self.dbg and self.stop == ('mixD', l):
            return
        def mch(n):
            return 16 + n if n < 8 else 32 + n

        wl = ws = None
        for n in range(16):
            nn = n % 2
            if nn == 0:
                wl = P.wnext(W['lruout'][n // 2, :, :], 2048)
                ws = P.wnext(W['scout'][n // 2, :, :], 2048)
            byl = P.bget()
            pairs = [(wl[0][:, (nn * 8 + kc) * 128:(nn * 8 + kc + 1) * 128], act[:, 24 + kc, :]) for kc in range(8)]
            self.mmg(byl.t[:, :], pairs, [wl[1]] + actB[24:32], [byl.b])
            bys = P.bget()
            pairs = [(ws[0][:, (nn * 8 + kc) * 128:(nn * 8 + kc + 1) * 128], act[:, 32 + kc, :]) for kc in range(8)]
            self.mmg(bys.t[:, :], pairs, [ws[1]] + actB[32:40], [bys.b])
            if nn == 1:
                P.wfree(wl)
                P.wfree(ws)
            bg0 = self.proj16(win, 69 + 3 * n)
            bg1 = self.proj16(win, 69 + 3 * n + 1)
            bg2 = self.proj16(win, 69 + 3 * n + 2)
            wm = P.wnext(W['mlaout'][n, :, :], 2048)
            bym = P.bget()
            pairs = [(wm[0][:, kc * 128:(kc + 1) * 128], act[:, kc, :]) for kc in range(16)]
            self.mmg(bym.t[:, :], pairs, [wm[1]] + actB[0:16], [bym.b])
            P.wfree(wm)
            s0 = P.sget()
            self.ACT(s0.t[:, 0:T], bg0.t[:, :], AF.Sigmoid, [bg0.b], [s0.b])
            P.bput(bg0)
            self.TT(s0.t[:, 0:T], s0.t[:, 0:T], byl.t[:, :], ALU.mult, [s0.b, byl.b], [s0.b])
            P.bput(byl)
            s1 = P.sget()
            self.ACT(s1.t[:, 0:T], bg1.t[:, :], AF.Sigmoid, [bg1.b], [s1.b])
            P.bput(bg1)
            self.TT(s1.t[:, 0:T], s1.t[:, 0:T], bys.t[:, :], ALU.mult, [s1.b, bys.b], [s1.b])
            P.bput(bys)
            self.TT(s0.t[:, 0:T], s0.t[:, 0:T], s1.t[:, 0:T], ALU.add, [s0.b, s1.b], [s0.b])
            P.sput(s1)
            s2 = P.sget()
            self.ACT(s2.t[:, 0:T], bg2.t[:, :], AF.Sigmoid, [bg2.b], [s2.b])
            P.bput(bg2)
            self.TT(s2.t[:, 0:T], s2.t[:, 0:T], bym.t[:, :], ALU.mult, [s2.b, bym.b], [s2.b])
            P.bput(bym)
            self.TT(act[:, mch(n), :], s0.t[:, 0:T], s2.t[:, 0:T], ALU.add, [s0.b, s2.b], [actB[mch(n)]])
            P.sput(s0)
            P.sput(s2)
            if n == 7:
                self.ada_bg(1)
        if self.dbg and self.stop == ('mixE', l):
            return
        self.ada_need(l, 5)
        self.preload_ln()
        Gm = mod[:, 80:96]
        self.xacc_begin()
        for n in range(16):
            w = P.wnext(W['wo'][n, :, :], 2048)
            bk = P.bget()
            pairs = [(w[0][:, kc * 128:(kc + 1) * 128], act[:, mch(kc), :]) for kc in range(16)]
            self.mmg(bk.t[:, :], pairs, [w[1]] + [actB[mch(kc)] for kc in range(16)], [bk.b])
            P.wfree(w)
            self.STT(self.xt[:, n, :], bk.t[:, :], Gm[:, n:n + 1], self.xt[:, n, :], ALU.mult, ALU.add,
                     [bk.b, self.xB[n], mB[5]], [self.xB[n]])
            P.bput(bk)
            self.xacc_add(n)

    def tile(self, t):
        P = self.P
        t0 = t * T
        src = self.xT[:, :].rearrange("(c p) t -> p c t", p=128)
        self.xacc_begin()
        for c in range(16):
            P.dma('sp', self.xt[:, c, :], src[:, c, t0:t0 + T], writes=[self.xB[c]])
            self.xacc_add(c)
        self.need_rope = t
        done = False
        for l in range(self.n_layers):
            self.cur_layer = l
            self.ffn(l, 0)
            if self.stop == ('ffn1', l):
                done = True
                break
            self.mixer(l, t)
            if self.stop is not None and self.stop[1] == l and self.stop[0] in ('mixD', 'mixE', 'mix'):
                done = True
                break
            self.ffn(l, 1)
            if self.stop == ('ffn2', l):
                done = True
                break
        if not done:
            dst = self.outT[:, :].rearrange("(c p) t -> p c t", p=128)
            rstd = self.acc_finish(self.xacc, 1.0 / D)
            for c in range(16):
                self.STT(self.xt[:, c, :], self.xt[:, c, :], self.prm_ap('fg', c), rstd.t[:, 0:T], ALU.mult, ALU.mult,
                         [self.xB[c], rstd.b, self.prmB], [self.xB[c]])
                tok = P.dma('sp', dst[:, c, t0:t0 + T], self.xt[:, c, :], reads=[self.xB[c]])
                if tok is not None:
                    self.out_toks.append(tok)
            P.sput(rstd)
        else:
            P.sput(self.xacc[0])
            dst = self.outT[:, :].rearrange("(c p) t -> p c t", p=128)[:, :, t0:t0 + T]
            tok = P.dma('sp', dst, self.xt[:, :, :], reads=self.xB)
            if tok is not None:
                self.out_toks.append(tok)
        if self.dbg:
            tok = P.dma('sp', self.dbgact[:, :], self.act[:, :, :].rearrange("p c t -> p (c t)"), reads=self.actB)
            if tok is not None:
                self.out_toks.append(tok)

    def build(self):
        P = self.P
        P.dry = True
        self.prologue()
        n0 = len(P.wseq)
        self.tile(0)
        n1 = len(P.wseq)
        assert self.stop is not None or not self.ada_pending
        if self.nt_run > 1:
            self.tile(1)
        seq = P.wseq[:n1] + P.wseq[n1:] * (self.nt_run - 1)
        P.wseq = seq
        P.dry = False
        P.wpump()
        self.prologue()
        for t in range(self.nt_run):
            self.tile(t)
        assert P.wi == len(P.wseq), (P.wi, len(P.wseq))
        P.wait_tokens('sp', self.out_toks)
        P.emit()


def _tile_units(Wm, ncols=128):
    K, N = Wm.shape
    kc = K // 128
    nu = N // ncols
    return np.ascontiguousarray(Wm.reshape(kc, 128, nu, ncols).transpose(2, 1, 0, 3)).reshape(nu, 128, kc * ncols)


def _vec16(v):
    n = v.shape[0] // 128
    return v.reshape(n, 128).T


def _win_cols():
    LX, LG, SB, SC, SX, Q, KV, KP, GT = 0, 1024, 2048, 3072, 4096, 5120, 8192, 8704, 8768
    cols = []
    r = np.arange
    for g in range(8):
        cols += [LX + g * 128 + r(128), LG + g * 128 + r(128)]
    for g in range(8):
        cols += [SC + g * 128 + r(128), SX + g * 128 + r(128), SB + g * 128 + r(128)]
    for c in range(4):
        cols.append(KV + c * 128 + r(128))
    x1, x2 = KP + r(32), KP + 32 + r(32)
    cols.append(np.concatenate([x1, x2, x1, x2]))
    for hp in range(8):
        h0, h1 = 2 * hp, 2 * hp + 1
        cols.append(Q + h0 * 192 + r(128))
        cols.append(Q + h1 * 192 + r(128))
        a1, a2 = Q + h0 * 192 + 128 + r(32), Q + h0 * 192 + 160 + r(32)
        b1, b2 = Q + h1 * 192 + 128 + r(32), Q + h1 * 192 + 160 + r(32)
        cols.append(np.concatenate([a1, a2, b1, b2]))
    for n in range(16):
        for br in range(3):
            cols.append(GT + br * 2048 + n * 128 + r(128))
    cols = np.concatenate(cols)
    assert cols.shape[0] == 117 * 128
    return cols


def prep_shared(inp):
    sh = {}
    cols = _win_cols()
    for l in range(DEPTH):
        sh["ada%d" % l] = _tile_units(inp["ada_w"][l])
        for f in range(2):
            w13 = inp["ffn_w13"][l, f]
            g = _tile_units(w13[:, :DFF])
            u = _tile_units(w13[:, DFF:])
            sh["w13_%d_%d" % (l, f)] = np.ascontiguousarray(np.stack([g, u], axis=1)).reshape(88, 128, 2048)
            sh["w2_%d_%d" % (l, f)] = _tile_units(inp["ffn_w2"][l, f])
        sh["win%d" % l] = _tile_units(inp["w_in"][l][:, cols])
        wa = inp["lru_wa"][l]
        wx = inp["lru_wx"][l]
        sh["lruw%d" % l] = np.ascontiguousarray(np.stack([wa, wx], axis=2)).reshape(8, 128, 256)
        for nm, key in (("lruout", "lru_out"), ("scout", "sc_out")):
            u = _tile_units(inp[key][l])
            sh["%s%d" % (nm, l)] = np.ascontiguousarray(u.reshape(8, 2, 128, 1024).transpose(0, 2, 1, 3)).reshape(8, 128, 2048)
        ukv = inp["mla_w_ukv"][l].reshape(4, 128, 4, 4, 2, 128)
        sh["ukvk%d" % l] = np.ascontiguousarray(ukv[:, :, :, :, 0, :].transpose(2, 1, 3, 0, 4)).reshape(4, 128, 2048)
        sh["ukvv%d" % l] = np.ascontiguousarray(ukv[:, :, :, :, 1, :].transpose(2, 1, 0, 3, 4)).reshape(4, 128, 2048)
        sh["mlaout%d" % l] = _tile_units(inp["mla_out"][l])
        sh["wo%d" % l] = _tile_units(inp["w_o"][l])
    perm = np.zeros((128, 128), np.float32)
    perm[np.arange(128) ^ 32, np.arange(128)] = 1.0
    sh["cst"] = np.ascontiguousarray(np.concatenate([np.triu(np.ones((128, 128), np.float32)), perm], axis=1))
    return sh


def prep_core(inp, b):
    prm = np.zeros((128, NPRM), np.float32)

    def put(name, arr):
        arr = np.asarray(arr, np.float32)
        prm[:, OFF[name]:OFF[name] + arr.shape[1]] = arr

    put('c', _vec16(inp["c"][b]))
    for l in range(DEPTH):
        put('ada_b%d' % l, _vec16(inp["ada_b"][l]))
        put('ng%d' % l, np.concatenate([_vec16(inp["norm_g"][l, i]) for i in range(3)], axis=1))
        put('cw%d' % l, inp["lru_conv_w"][l].reshape(4, 8, 128).transpose(2, 1, 0).reshape(128, 32))
        put('cb%d' % l, _vec16(inp["lru_conv_b"][l]))
        put('ba%d' % l, _vec16(inp["lru_ba"][l]))
        put('bx%d' % l, _vec16(inp["lru_bx"][l]))
        put('lam%d' % l, _vec16(inp["lru_lambda"][l]))
        put('scw%d' % l, inp["sc_conv_w"][l].reshape(3, 8, 128).transpose(2, 1, 0).reshape(128, 24))
        put('kvg%d' % l, _vec16(inp["mla_kv_norm_g"][l]))
    put('fg', _vec16(inp["final_norm_g"]))
    half = 32
    inv = (np.float32(10000.0) ** (-np.arange(half, dtype=np.float32) / np.float32(half))).astype(np.float32)
    put('invf', np.tile(inv, 4)[:, None])
    put('sgn', np.tile(np.concatenate([-np.ones(32, np.float32), np.ones(32, np.float32)]), 2)[:, None])
    m = {"prm": prm,
         "xT": np.ascontiguousarray(inp["x"][b].T),
         "pos": np.ascontiguousarray(np.broadcast_to(inp["positions"][b].astype(np.int32)[None, :], (128, S)))}
    return m


_CACHE = {}


def get_nc(nt_run=NT, n_layers=DEPTH, stop=None, dbg=False):
    key = (nt_run, n_layers, stop, dbg)
    if key not in _CACHE:
        nc = bass.Bass("TRN2", target_bir_lowering=False)
        g = Gen(nc, nt_run, n_layers, stop, dbg)
        g.build()
        _CACHE[key] = nc
    return _CACHE[key]


def kernel(**inputs):
    inp = {k: np.asarray(v) for k, v in inputs.items()}
    B = inp["x"].shape[0]
    sh = prep_shared(inp)
    in_maps = []
    for b in range(B):
        m = prep_core(inp, b)
        m.update(sh)
        in_maps.append(m)
    nc = get_nc()
    res = run_bass_kernel_spmd(nc, in_maps, core_ids=list(range(B)))
    out = np.stack([np.ascontiguousarray(res.results[b]["outT"].T) for b in range(B)], axis=0)
    return out.astype(np.float32)
```

```python
import numpy as np
import concourse.bass as bass
import concourse.mybir as mybir
from concourse.bass_utils import run_bass_kernel_spmd

F32 = mybir.dt.float32
BF16 = mybir.dt.bfloat16
I32 = mybir.dt.int32
AF = mybir.ActivationFunctionType
ALU = mybir.AluOpType

D = 2048
S = 2048
T = 512
NT = S // T
DEPTH = 2
DFF = 5632
NFF = DFF // 128
EPS = 1e-6
NS = 7
NSCR = 9
SCALE = float((128 + 64) ** -0.5)
TWO_PI = 6.283185307179586
ENGS = ['pe', 'act', 'dve', 'pool', 'sp']
STRICT_WAR = True

OFF = {}
_o = 0


def _add(name, n):
    global _o
    OFF[name] = _o
    _o += n


_add('c', 16)
for _l in range(DEPTH):
    _add('ada_b%d' % _l, 144)
    _add('ng%d' % _l, 48)
    _add('cw%d' % _l, 32)
    _add('cb%d' % _l, 8)
    _add('ba%d' % _l, 8)
    _add('bx%d' % _l, 8)
    _add('lam%d' % _l, 8)
    _add('scw%d' % _l, 24)
    _add('kvg%d' % _l, 4)
_add('fg', 16)
_add('invf', 1)
_add('sgn', 1)
NPRM = _o


class Buf:
    __slots__ = ('name', 'w', 'r')

    def __init__(self, name=''):
        self.name = name
        self.w = None
        self.r = {}


class Res:
    __slots__ = ('t', 'b')

    def __init__(self, t, b):
        self.t = t
        self.b = b


class Prog:
    def __init__(self, nc, ndma=32):
        self.nc = nc
        self.q = {e: [] for e in ENGS}
        self.sem = {e: nc.alloc_semaphore(name="s_" + e) for e in ENGS}
        self.cnt = {e: 0 for e in ENGS}
        self.waited = {e: {} for e in ENGS}
        self.dsem = [nc.alloc_semaphore(name="d_%d" % i) for i in range(ndma)]
        self.dcnt = [0] * ndma
        self.dpool = {'pool': list(range(0, ndma // 2)), 'sp': list(range(ndma // 2, ndma))}
        self.drr = {'pool': 0, 'sp': 0}
        self.dry = False
        self.wseq = []
        self.wi = 0
        self.wissued = 0
        self.wconsumed = 0
        self.wdone = set()
        self.slots = [nc.alloc_sbuf_tensor("wslot%d" % i, [128, 2048], BF16) for i in range(NS)]
        self.slotB = [Buf("slot%d" % i) for i in range(NS)]
        self.banks = [Res(nc.alloc_psum_tensor("bank%d" % i, [128, 512], F32), Buf("bank%d" % i)) for i in range(8)]
        self.bfree = list(self.banks)
        self.scr = [Res(nc.alloc_sbuf_tensor("scr%d" % i, [128, 516], F32), Buf("scr%d" % i)) for i in range(NSCR)]
        self.sfree = list(self.scr)
        self.pts = [Res(nc.alloc_sbuf_tensor("pt%d" % i, [128, 512], BF16), Buf("pt%d" % i)) for i in range(4)]
        self.pfree = list(self.pts)

    def bget(self):
        return self.bfree.pop(0)

    def bput(self, r):
        self.bfree.append(r)

    def sget(self):
        return self.sfree.pop(0)

    def sput(self, r):
        self.sfree.append(r)

    def pget(self):
        return self.pfree.pop(0)

    def pput(self, r):
        self.pfree.append(r)

    def _handle(self, key):
        return self.sem[key] if isinstance(key, str) else self.dsem[key]

    def _need(self, eng, toks):
        wd = self.waited[eng]
        best = {}
        for (k, v) in toks:
            if k == eng and eng == 'pe':
                continue
            if wd.get(k, 0) < v and best.get(k, 0) < v:
                best[k] = v
        for k, v in best.items():
            wd[k] = v
        return list(best.items())

    def _deps(self, eng, reads, writes):
        toks = []
        for b in reads:
            if b.w is not None:
                toks.append(b.w)
        for b in writes:
            if b.w is not None:
                toks.append(b.w)
            for k, v in b.r.items():
                if k != eng or STRICT_WAR:
                    toks.append((k, v))
        return toks

    def _mark(self, tok, reads, writes):
        k, v = tok
        for b in reads:
            b.r[k] = v
        for b in writes:
            b.w = tok
            b.r = {}

    def op(self, eng, fn, reads=(), writes=()):
        if self.dry:
            return
        waits = self._need(eng, self._deps(eng, reads, writes))
        self.cnt[eng] += 1
        tok = (eng, self.cnt[eng])
        self.q[eng].append((waits, fn, eng))
        self._mark(tok, reads, writes)

    def dma(self, eng, out, in_, reads=(), writes=()):
        if self.dry:
            return None
        lst = self.dpool[eng]
        k = lst[self.drr[eng]]
        self.drr[eng] = (self.drr[eng] + 1) % len(lst)
        toks = self._deps(eng, reads, writes)
        if self.dcnt[k] > 0:
            toks.append((k, self.dcnt[k]))
        waits = self._need(eng, toks)
        self.dcnt[k] += 16
        tok = (k, self.dcnt[k])
        self.q[eng].append((waits, lambda e: e.dma_start(out=out, in_=in_), k))
        self._mark(tok, reads, writes)
        return tok

    def wait_tokens(self, eng, toks):
        waits = self._need(eng, toks)
        if waits:
            self.q[eng].append((waits, None, None))

    def wnext(self, ap, n):
        if self.dry:
            self.wseq.append((ap, n))
            return (self.slots[0], self.slotB[0], -1)
        i = self.wi
        self.wi += 1
        assert i < self.wissued, "weight stream underrun %d %d" % (i, self.wissued)
        assert self.wseq[i][1] == n
        s = i % NS
        return (self.slots[s], self.slotB[s], i)

    def wfree(self, w):
        if self.dry:
            return
        self.wdone.add(w[2])
        while self.wconsumed in self.wdone:
            self.wdone.discard(self.wconsumed)
            self.wconsumed += 1
        self.wpump()

    def wpump(self):
        limit = min(len(self.wseq), self.wconsumed + NS)
        while self.wissued < limit:
            j = self.wissued
            s = j % NS
            ap, n = self.wseq[j]
            self.dma('pool', self.slots[s][:, 0:n], ap, writes=[self.slotB[s]])
            self.wissued += 1

    def emit(self):
        nc = self.nc
        prog = self

        def run(engname, e):
            for waits, fn, inc in prog.q[engname]:
                for k, v in waits:
                    e.wait_ge(prog._handle(k), v)
                if fn is None:
                    continue
                ins = fn(e)
                if isinstance(inc, str):
                    ins.then_inc(prog.sem[inc], 1)
                else:
                    ins.then_inc(prog.dsem[inc], 16)

        with nc.Block() as block:
            @block.tensor
            def _(e):
                run('pe', e)

            @block.scalar
            def _(e):
                run('act', e)

            @block.vector
            def _(e):
                run('dve', e)

            @block.gpsimd
            def _(e):
                run('pool', e)

            @block.sync
            def _(e):
                run('sp', e)


class Gen:
    def __init__(self, nc, nt_run=NT, n_layers=DEPTH, stop=None, dbg=False):
        self.nc = nc
        self.nt_run = nt_run
        self.n_layers = n_layers
        self.stop = stop
        self.dbg = dbg
        dt = nc.dram_tensor
        self.xT = dt("xT", [D, S], F32, kind="ExternalInput")
        self.pos = dt("pos", [128, S], I32, kind="ExternalInput")
        self.prm = dt("prm", [128, NPRM], F32, kind="ExternalInput")
        self.cst = dt("cst", [128, 256], F32, kind="ExternalInput")
        self.W = []
        for l in range(DEPTH):
            w = {}
            w['ada'] = dt("ada%d" % l, [144, 128, 2048], F32, kind="ExternalInput")
            for f in range(2):
                w['w13_%d' % f] = dt("w13_%d_%d" % (l, f), [88, 128, 2048], F32, kind="ExternalInput")
                w['w2_%d' % f] = dt("w2_%d_%d" % (l, f), [16, 128, NFF * 128], F32, kind="ExternalInput")
            w['win'] = dt("win%d" % l, [117, 128, 2048], F32, kind="ExternalInput")
            w['lruw'] = dt("lruw%d" % l, [8, 128, 256], F32, kind="ExternalInput")
            w['lruout'] = dt("lruout%d" % l, [8, 128, 2048], F32, kind="ExternalInput")
            w['scout'] = dt("scout%d" % l, [8, 128, 2048], F32, kind="ExternalInput")
            w['ukvk'] = dt("ukvk%d" % l, [4, 128, 2048], F32, kind="ExternalInput")
            w['ukvv'] = dt("ukvv%d" % l, [4, 128, 2048], F32, kind="ExternalInput")
            w['mlaout'] = dt("mlaout%d" % l, [16, 128, 2048], F32, kind="ExternalInput")
            w['wo'] = dt("wo%d" % l, [16, 128, 2048], F32, kind="ExternalInput")
            self.W.append(w)
        self.outT = dt("outT", [D, S], F32, kind="ExternalOutput")
        self.kc = [[dt("kc_%d_%d" % (l, h), [128, S], BF16) for h in range(16)] for l in range(DEPTH)]
        self.vc = [[dt("vc_%d_%d" % (l, h), [16, 128, 128], BF16) for h in range(16)] for l in range(DEPTH)]
        self.kcB = [[Buf() for h in range(16)] for l in range(DEPTH)]
        self.vcB = [[Buf() for h in range(16)] for l in range(DEPTH)]
        if dbg:
            self.dbgact = dt("dbgact", [128, 48 * T], BF16, kind="ExternalOutput")

        P = self.P = Prog(nc)
        sb = nc.alloc_sbuf_tensor
        self.xt = sb("xt", [128, 16, T], F32)
        self.xB = [Buf("x%d" % c) for c in range(16)]
        self.hT = sb("hT", [128, 16, T], BF16)
        self.hB = [Buf("h%d" % c) for c in range(16)]
        self.act = sb("act", [128, 48, T], BF16)
        self.actB = [Buf("act%d" % c) for c in range(48)]
        self.ckv = [sb("ckv%d" % l, [128, 4, S], BF16) for l in range(DEPTH)]
        self.ckvB = [Buf("ckv%d" % l) for l in range(DEPTH)]
        self.kpe = [sb("kpe%d" % l, [128, S], BF16) for l in range(DEPTH)]
        self.kpeB = [Buf("kpe%d" % l) for l in range(DEPTH)]
        self.kn = sb("kn", [128, S], BF16)
        self.knB = [Buf("kn%d" % i) for i in range(NT)]
        self.vg = sb("vg", [128, S], BF16)
        self.vgB = [Buf("vg%d" % i) for i in range(NT)]
        self.prm_sb = sb("prm_sb", [128, NPRM], F32)
        self.prmB = Buf("prm")
        self.mod = [sb("mod%d" % l, [128, 144], F32) for l in range(DEPTH)]
        self.vec = [sb("vec%d" % l, [128, 96], F32) for l in range(DEPTH)]
        self.vecB = Buf("vec")
        self.modB = [[Buf("mod%d_%d" % (l, j)) for j in range(9)] for l in range(DEPTH)]
        self.lamB = Buf("lam")
        self.dummy = sb("dummy_ln", [128, 2], F32)
        self.dummyB = Buf("dummy")
        self.need_rope = None
        self.cur_layer = 0
        self.cactB = Buf("cact")
        self.ada_pending = []
        self.ada_bank = None
        self.halo = [sb("halo%d" % l, [128, 24], F32) for l in range(DEPTH)]
        self.shalo = [sb("shalo%d" % l, [128, 16], F32) for l in range(DEPTH)]
        self.hst = [sb("hst%d" % l, [128, 8], F32) for l in range(DEPTH)]
        self.haloB = [Buf() for l in range(DEPTH)]
        self.shaloB = [Buf() for l in range(DEPTH)]
        self.hstB = [Buf() for l in range(DEPTH)]
        self.ones32 = sb("ones32", [128, 128], F32)
        self.onesbf = sb("onesbf", [128, 128], BF16)
        self.mask = sb("mask", [128, 128], BF16)
        self.perm = sb("perm", [128, 128], F32)
        self.cact = sb("cact", [128, 16], BF16)
        self.cstB = Buf("cst")
        self.posi = sb("posi", [128, T], I32)
        self.posiB = Buf("posi")
        self.cos_t = sb("cos_t", [128, T], F32)
        self.sin_t = sb("sin_t", [128, T], F32)
        self.ropeB = Buf("rope")
        self.out_toks = []

    def prm_ap(self, name, a=0, b=None):
        o = OFF[name]
        if b is None:
            b = a + 1
        return self.prm_sb[:, o + a:o + b]

    def mmg(self, out_ap, pairs, reads, writes, start=True, stop=True):
        n = len(pairs)

        def fn(e):
            ins = None
            for i, (l, r) in enumerate(pairs):
                ins = e.matmul(out_ap, lhsT=l, rhs=r, start=(start and i == 0), stop=(stop and i == n - 1))
            return ins
        self.P.op('pe', fn, reads, writes)

    def ACT(self, out, in_, func, reads, writes, **kw):
        self.P.op('act', lambda e: e.activation(out=out, in_=in_, func=func, **kw), reads, writes)

    def TT(self, out, a, b, op, reads, writes):
        self.P.op('dve', lambda e: e.tensor_tensor(out=out, in0=a, in1=b, op=op), reads, writes)

    def TS(self, out, a, s1, s2, op0, op1, reads, writes):
        self.P.op('dve', lambda e: e.tensor_scalar(out=out, in0=a, scalar1=s1, scalar2=s2, op0=op0, op1=op1), reads, writes)

    def STT(self, out, a, s, b, op0, op1, reads, writes):
        self.P.op('dve', lambda e: e.scalar_tensor_tensor(out=out, in0=a, scalar=s, in1=b, op0=op0, op1=op1), reads, writes)

    def CP(self, eng, out, in_, reads, writes):
        if eng == 'act':
            self.P.op('act', lambda e: e.activation(out=out, in_=in_, func=AF.Copy), reads, writes)
        else:
            self.P.op(eng, lambda e: e.tensor_copy(out=out, in_=in_), reads, writes)

    def proj16(self, wdram, idx):
        P = self.P
        w = P.wnext(wdram[idx, :, :], 2048)
        bk = P.bget()
        pairs = [(w[0][:, kc * 128:(kc + 1) * 128], self.hT[:, kc, :]) for kc in range(16)]
        self.mmg(bk.t[:, :], pairs, [w[1]] + self.hB, [bk.b])
        P.wfree(w)
        return bk

    def acc_begin(self):
        return [self.P.sget(), 0]

    def acc_add(self, st, ap, b, last=False):
        P = self.P
        acc = st[0]
        if last and st[1] > 0:
            sq = P.sget()
            self.ACT(sq.t[:, 0:T], ap, AF.Square, [b], [sq.b])
            st.append(sq)
            st[1] += 1
            return
        if st[1] == 0:
            self.ACT(acc.t[:, 0:T], ap, AF.Square, [b], [acc.b])
        else:
            sq = P.sget()
            self.ACT(sq.t[:, 0:T], ap, AF.Square, [b], [sq.b])
            self.TT(acc.t[:, 0:T], acc.t[:, 0:T], sq.t[:, 0:T], ALU.add, [acc.b, sq.b], [acc.b])
            P.sput(sq)
        st[1] += 1

    def acc_finish(self, st, inv_n):
        P = self.P
        acc = st[0]
        bank = P.bget()
        if len(st) > 2:
            sq = st[2]
            self.mmg(bank.t[:, :], [(self.ones32[:, :], acc.t[:, 0:T])], [acc.b, self.cstB], [bank.b],
                     start=True, stop=False)
            self.mmg(bank.t[:, :], [(self.ones32[:, :], sq.t[:, 0:T])], [sq.b, self.cstB], [bank.b],
                     start=False, stop=True)
            P.sput(sq)
        else:
            self.mmg(bank.t[:, :], [(self.ones32[:, :], acc.t[:, 0:T])], [acc.b, self.cstB], [bank.b])
        self.ACT(acc.t[:, 0:T], bank.t[:, :], AF.Ln, [bank.b], [acc.b], scale=inv_n, bias=EPS)
        P.bput(bank)
        self.ACT(acc.t[:, 0:T], acc.t[:, 0:T], AF.Exp, [acc.b], [acc.b], scale=-0.5)
        return acc

    def preload_ln(self):
        self.ACT(self.dummy[:, 0:1], self.dummy[:, 1:2], AF.Ln, [self.dummyB], [self.dummyB])

    def xacc_begin(self):
        self.xacc = self.acc_begin()

    def xacc_add(self, c):
        self.acc_add(self.xacc, self.xt[:, c, :], self.xB[c], last=(c == 15))

    def norm_mod(self, A, B, AB, BB):
        P = self.P
        rstd = self.acc_finish(self.xacc, 1.0 / D)
        for c in range(16):
            tmp = P.sget()
            self.STT(tmp.t[:, 0:T], self.xt[:, c, :], A[:, c:c + 1], rstd.t[:, 0:T], ALU.mult, ALU.mult,
                     [self.xB[c], rstd.b, AB], [tmp.b])
            self.ACT(self.hT[:, c, :], tmp.t[:, 0:T], AF.Identity, [tmp.b, BB], [self.hB[c]],
                     bias=B[:, c:c + 1], scale=1.0)
            P.sput(tmp)
        P.sput(rstd)

    def proj_pair_kouter(self, wdA, iA, wdB, iB):
        P = self.P
        wA = P.wnext(wdA[iA, :, :], 2048)
        wB = P.wnext(wdB[iB, :, :], 2048)
        bA = P.bget()
        bB = P.bget()
        for kc in range(16):
            for (w, bk) in ((wA, bA), (wB, bB)):
                P.op('pe', lambda e, o=bk.t[:, :], l_=w[0][:, kc * 128:(kc + 1) * 128], r_=self.hT[:, kc, :],
                     st=(kc == 0), sp_=(kc == 15): e.matmul(o, lhsT=l_, rhs=r_, start=st, stop=sp_),
                     [w[1], self.hB[kc]], [bk.b])
        P.wfree(wA)
        P.wfree(wB)
        return bA, bB

    def ada_group(self, l, j):
        P = self.P
        bank = P.bget()
        for jj in range(16):
            w = P.wnext(self.W[l]['ada'][16 * j + jj, :, :], 2048)
            pairs = [(w[0][:, kc * 128:(kc + 1) * 128], self.cact[:, kc:kc + 1]) for kc in range(16)]
            self.mmg(bank.t[:, jj:jj + 1], pairs, [w[1], self.cactB], [bank.b])
            P.wfree(w)
        mod = self.mod[l]
        vec = self.vec[l]
        mB = self.modB[l][j]
        self.TT(mod[:, 16 * j:16 * j + 16], bank.t[:, 0:16], self.prm_ap('ada_b%d' % l, 16 * j, 16 * j + 16), ALU.add,
                [bank.b, self.prmB], [mB])
        P.bput(bank)
        ng = 'ng%d' % l
        rw = [mB, self.prmB]
        if j == 1:
            self.STT(vec[:, 0:16], mod[:, 16:32], 1.0, self.prm_ap(ng, 0, 16), ALU.add, ALU.mult, rw, [mB])
        elif j == 2:
            self.TS(vec[:, 16:32], mod[:, 32:48], 0.5, 0.0, ALU.mult, ALU.add, rw, [mB])
        elif j == 4:
            self.STT(vec[:, 32:48], mod[:, 64:80], 1.0, self.prm_ap(ng, 16, 32), ALU.add, ALU.mult, rw, [mB])
        elif j == 7:
            self.STT(vec[:, 48:64], mod[:, 112:128], 1.0, self.prm_ap(ng, 32, 48), ALU.add, ALU.mult, rw, [mB])
        elif j == 8:
            self.TS(vec[:, 64:80], mod[:, 128:144], 0.5, 0.0, ALU.mult, ALU.add, rw, [mB])

    def ada_need(self, l, j):
        while (l, j) in self.ada_pending:
            self.ada_group(*self.ada_pending.pop(0))

    def ada_bg(self, k=1):
        for _ in range(k):
            if self.ada_pending:
                l, j = self.ada_pending[0]
                if l == self.cur_layer or (l == self.cur_layer + 1 and j <= 1):
                    self.ada_group(*self.ada_pending.pop(0))

    def prologue(self):
        P = self.P
        nc = self.nc
        P.dma('sp', self.prm_sb[:, :], self.prm[:, :], writes=[self.prmB])
        P.dma('pool', self.mask[:, :], self.cst[:, 0:128], writes=[self.cstB])
        P.dma('sp', self.perm[:, :], self.cst[:, 128:256], writes=[self.cstB])
        P.op('dve', lambda e: e.memset(self.ones32[:, :], 1.0), [], [self.cstB])
        P.op('dve', lambda e: e.memset(self.onesbf[:, :], 1.0), [], [self.cstB])
        P.op('dve', lambda e: e.memset(self.dummy[:, :], 1.0), [], [self.dummyB])
        for l in range(DEPTH):
            P.op('dve', lambda e, l=l: e.memset(self.halo[l][:, :], 0.0), [], [self.haloB[l]])
            P.op('dve', lambda e, l=l: e.memset(self.shalo[l][:, :], 0.0), [], [self.shaloB[l]])
            P.op('dve', lambda e, l=l: e.memset(self.hst[l][:, :], 0.0), [], [self.hstB[l]])
        self.ACT(self.cact[:, :], self.prm_ap('c', 0, 16), AF.Silu, [self.prmB], [self.cactB])
        for l in range(self.n_layers):
            vec = self.vec[l]
            rw = [self.lamB, self.prmB]
            self.ACT(vec[:, 80:88], self.prm_ap('lam%d' % l, 0, 8), AF.Exp, rw, [self.lamB], scale=-1.0)
            self.ACT(vec[:, 80:88], vec[:, 80:88], AF.Ln, rw, [self.lamB], bias=1.0)
            self.TS(vec[:, 80:88], vec[:, 80:88], -8.0, 0.0, ALU.mult, ALU.add, rw, [self.lamB])
            self.TS(vec[:, 88:96], vec[:, 80:88], 2.0, 0.0, ALU.mult, ALU.add, rw, [self.lamB])
        self.ada_pending = [(l, j) for l in range(self.n_layers) for j in range(9)]

    def rope_tables(self, t):
        P = self.P
        t0 = t * T
        P.dma('sp', self.posi[:, :], self.pos[:, t0:t0 + T], writes=[self.posiB])
        turns = P.sget()
        self.CP('dve', turns.t[:, 0:T], self.posi[:, :], [self.posiB], [turns.b])
        self.TS(turns.t[:, 0:T], turns.t[:, 0:T], self.prm_ap('invf'), 1.0 / TWO_PI, ALU.mult, ALU.mult,
                [turns.b, self.prmB], [turns.b])
        for which in range(2):
            tc = P.sget()
            self.TS(tc.t[:, 0:T], turns.t[:, 0:T], 0.25 if which == 1 else 0.0, 0.0, ALU.add, ALU.add,
                    [turns.b], [tc.b])
            self.CP('dve', self.posi[:, :], tc.t[:, 0:T], [tc.b], [self.posiB])
            rf = P.sget()
            self.CP('dve', rf.t[:, 0:T], self.posi[:, :], [self.posiB], [rf.b])
            self.TT(rf.t[:, 0:T], tc.t[:, 0:T], rf.t[:, 0:T], ALU.subtract, [tc.b, rf.b], [rf.b])
            dst = self.cos_t if which == 1 else self.sin_t
            self.ACT(dst[:, :], rf.t[:, 0:T], AF.Sin, [rf.b], [self.ropeB], scale=TWO_PI * (1.0 - 1e-6))
            P.sput(tc)
            P.sput(rf)
        P.sput(turns)
        self.TS(self.sin_t[:, :], self.sin_t[:, :], self.prm_ap('sgn'), 0.0, ALU.mult, ALU.add,
                [self.ropeB, self.prmB], [self.ropeB])

    def rope_apply(self, bA, dst, dstB):
        P = self.P
        sA = P.sget()
        self.CP('act', sA.t[:, 0:T], bA.t[:, :], [bA.b], [sA.b])
        P.bput(bA)
        bB = P.bget()
        self.mmg(bB.t[:, :], [(self.perm[:, :], sA.t[:, 0:T])], [self.cstB, sA.b], [bB.b])
        t1 = P.sget()
        t2 = P.sget()
        self.TT(t1.t[:, 0:T], sA.t[:, 0:T], self.cos_t[:, :], ALU.mult, [sA.b, self.ropeB], [t1.b])
        self.TT(t2.t[:, 0:T], bB.t[:, :], self.sin_t[:, :], ALU.mult, [bB.b, self.ropeB], [t2.b])
        P.sput(sA)
        P.bput(bB)
        self.TT(dst, t1.t[:, 0:T], t2.t[:, 0:T], ALU.add, [t1.b, t2.b], [dstB])
        P.sput(t1)
        P.sput(t2)

    def ffn(self, l, f):
        P = self.P
        vec = self.vec[l]
        mod = self.mod[l]
        mB = self.modB[l]
        if f == 0:
            jB, jA, jG = 0, 1, 2
            A, B, G = vec[:, 0:16], mod[:, 0:16], vec[:, 16:32]
        else:
            jB, jA, jG = 6, 7, 8
            A, B, G = vec[:, 48:64], mod[:, 96:112], vec[:, 64:80]
        self.ada_need(l, jB)
        self.ada_need(l, jA)
        self.norm_mod(A, B, mB[jA], mB[jB])
        w13 = self.W[l]['w13_%d' % f]
        w2 = self.W[l]['w2_%d' % f]
        for j in range(NFF):
            if j == 0:
                bg, bu = self.proj_pair_kouter(w13, 0, w13, 1)
            else:
                bg = self.proj16(w13, 2 * j)
                bu = self.proj16(w13, 2 * j + 1)
            sg = P.sget()
            self.ACT(sg.t[:, 0:T], bg.t[:, :], AF.Silu, [bg.b], [sg.b])
            P.bput(bg)
            self.TT(self.act[:, j, :], sg.t[:, 0:T], bu.t[:, :], ALU.mult, [sg.b, bu.b], [self.actB[j]])
            P.bput(bu)
            P.sput(sg)
            if j in (14, 29):
                self.ada_bg(1)
            if j == 3 and self.need_rope is not None:
                self.rope_tables(self.need_rope)
                self.need_rope = None
        self.preload_ln()
        self.ada_need(l, jG)
        parts = [(0, 16), (16, 32), (32, NFF)]
        self.xacc_begin()
        for n in range(16):
            units = [P.wnext(w2[n, :, k0 * 128:k1 * 128], (k1 - k0) * 128) for (k0, k1) in parts]
            by = P.bget()
            pairs = []
            for (k0, k1), w in zip(parts, units):
                for kc in range(k0, k1):
                    pairs.append((w[0][:, (kc - k0) * 128:(kc - k0 + 1) * 128], self.act[:, kc, :]))
            self.mmg(by.t[:, :], pairs, [w[1] for w in units] + self.actB[0:NFF], [by.b])
            for w in units:
                P.wfree(w)
            self.STT(self.xt[:, n, :], by.t[:, :], G[:, n:n + 1], self.xt[:, n, :], ALU.mult, ALU.add,
                     [by.b, self.xB[n], mB[jG]], [self.xB[n]])
            P.bput(by)
            self.xacc_add(n)
            if n == 7:
                self.ada_bg(1)

    def attn_K(self, l, t, h, uk):
        P = self.P
        hh = h % 4
        ckv = self.ckv[l]
        if t > 0:
            P.dma('sp', self.kn[:, 0:t * T], self.kc[l][h][:, 0:t * T],
                  reads=[self.kcB[l][h]], writes=self.knB[0:t])
        bk = P.bget()
        pairs = [(uk[0][:, (hh * 4 + kc) * 128:(hh * 4 + kc + 1) * 128], ckv[:, kc, t * T:(t + 1) * T])
                 for kc in range(4)]
        self.mmg(bk.t[:, :], pairs, [uk[1], self.ckvB[l]], [bk.b])
        self.CP('dve', self.kn[:, t * T:(t + 1) * T], bk.t[:, :], [bk.b], [self.knB[t]])
        P.bput(bk)
        if t < self.nt_run - 1:
            P.dma('sp', self.kc[l][h][:, t * T:(t + 1) * T], self.kn[:, t * T:(t + 1) * T],
                  reads=[self.knB[t]], writes=[self.kcB[l][h]])

    def attn_V(self, l, t, h, uv):
        P = self.P
        hh = h % 4
        ckv = self.ckv[l]
        vc3 = self.vc[l][h][:, :, :].rearrange("k p d -> p k d")
        if t > 0:
            P.dma('sp', self.vg[:, 0:t * T].rearrange("p (k d) -> p k d", d=128), vc3[:, 0:4 * t, :],
                  reads=[self.vcB[l][h]], writes=self.vgB[0:t])
        bv = P.bget()
        for i in range(4):
            kb = t * 4 + i
            pairs = [(ckv[:, kc, kb * 128:(kb + 1) * 128],
                      uv[0][:, (kc * 4 + hh) * 128:(kc * 4 + hh + 1) * 128]) for kc in range(4)]
            self.mmg(bv.t[:, i * 128:(i + 1) * 128], pairs, [uv[1], self.ckvB[l]], [bv.b])
        self.CP('dve', self.vg[:, t * T:(t + 1) * T], bv.t[:, :], [bv.b], [self.vgB[t]])
        P.bput(bv)
        if t < self.nt_run - 1:
            P.dma('sp', vc3[:, 4 * t:4 * t + 4, :], self.vg[:, t * T:(t + 1) * T].rearrange("p (k d) -> p k d", d=128),
                  reads=[self.vgB[t]], writes=[self.vcB[l][h]])

    def mixer(self, l, t):
        P = self.P
        W = self.W[l]
        win = W['win']
        vec = self.vec[l]
        mod = self.mod[l]
        mB = self.modB[l]
        t0 = t * T
        act = self.act
        actB = self.actB
        self.ada_need(l, 3)
        self.ada_need(l, 4)
        self.norm_mod(vec[:, 32:48], mod[:, 48:64], mB[4], mB[3])
        rwv = [self.lamB, self.prmB]
        cw = 'cw%d' % l
        scw = 'scw%d' % l
        def sc_g(g):
            bc = self.proj16(win, 16 + 3 * g)
            cs = P.sget()
            self.CP('act', cs.t[:, 0:T], bc.t[:, :], [bc.b], [cs.b])
            P.bput(bc)
            bxx = self.proj16(win, 16 + 3 * g + 1)
            sxw = P.sget()
            self.CP('act', sxw.t[:, 0:2], self.shalo[l][:, g * 2:g * 2 + 2], [self.shaloB[l]], [sxw.b])
            self.TT(sxw.t[:, 2:2 + T], cs.t[:, 0:T], bxx.t[:, :], ALU.mult, [cs.b, bxx.b], [sxw.b])
            P.bput(bxx)
            P.sput(cs)
            self.CP('act', self.shalo[l][:, g * 2:g * 2 + 2], sxw.t[:, T:T + 2], [sxw.b], [self.shaloB[l]])
            v = P.sget()
            self.TS(v.t[:, 0:T], sxw.t[:, 2:2 + T], self.prm_ap(scw, g * 3 + 2), 0.0, ALU.mult, ALU.add,
                    [sxw.b] + rwv, [v.b])
            for k in range(2):
                self.STT(v.t[:, 0:T], sxw.t[:, k:k + T], self.prm_ap(scw, g * 3 + k), v.t[:, 0:T],
                         ALU.mult, ALU.add, [sxw.b, v.b] + rwv, [v.b])
            P.sput(sxw)
            bb = self.proj16(win, 16 + 3 * g + 2)
            self.TT(act[:, 32 + g, :], v.t[:, 0:T], bb.t[:, :], ALU.mult, [v.b, bb.b], [actB[32 + g]])
            P.bput(bb)
            P.sput(v)
            if g == 7:
                self.ada_bg(1)
        for g in range(8):
            if g == 0:
                bx, bgt = self.proj_pair_kouter(win, 0, win, 1)
            else:
                bx = self.proj16(win, 2 * g)
                bgt = self.proj16(win, 2 * g + 1)
            lxw = P.sget()
            self.CP('act', lxw.t[:, 0:3], self.halo[l][:, g * 3:g * 3 + 3], [self.haloB[l]], [lxw.b])
            self.CP('act', lxw.t[:, 3:3 + T], bx.t[:, :], [bx.b], [lxw.b])
            P.bput(bx)
            self.CP('act', self.halo[l][:, g * 3:g * 3 + 3], lxw.t[:, T:T + 3], [lxw.b], [self.haloB[l]])
            gx = P.sget()
            self.CP('act', gx.t[:, 0:T], bgt.t[:, :], [bgt.b], [gx.b])
            P.bput(bgt)
            xr = P.sget()
            self.TS(xr.t[:, 0:T], lxw.t[:, 3:3 + T], self.prm_ap(cw, g * 4 + 3), self.prm_ap('cb%d' % l, g),
                    ALU.mult, ALU.add, [lxw.b] + rwv, [xr.b])
            for k in range(3):
                self.STT(xr.t[:, 0:T], lxw.t[:, k:k + T], self.prm_ap(cw, g * 4 + k), xr.t[:, 0:T],
                         ALU.mult, ALU.add, [lxw.b, xr.b] + rwv, [xr.b])
            P.sput(lxw)
            xrb = P.pget()
            self.CP('act', xrb.t[:, :], xr.t[:, 0:T], [xr.b], [xrb.b])
            sc_g(g)
            lw = P.wnext(W['lruw'][g, :, :], 256)
            br = P.bget()
            self.mmg(br.t[:, :], [(lw[0][:, 0:128], xrb.t[:, :])], [lw[1], xrb.b], [br.b])
            bi = P.bget()
            self.mmg(bi.t[:, :], [(lw[0][:, 128:256], xrb.t[:, :])], [lw[1], xrb.b], [bi.b])
            P.wfree(lw)
            P.pput(xrb)
            u = P.sget()
            self.TT(u.t[:, 0:T], gx.t[:, 0:T], gx.t[:, 0:T], ALU.mult, [gx.b], [u.b])
            self.TS(u.t[:, 0:T], u.t[:, 0:T], 0.044715, 1.0, ALU.mult, ALU.add, [u.b], [u.b])
            self.TT(u.t[:, 0:T], u.t[:, 0:T], gx.t[:, 0:T], ALU.mult, [u.b, gx.b], [u.b])
            self.ACT(u.t[:, 0:T], u.t[:, 0:T], AF.Sigmoid, [u.b], [u.b], scale=1.5957691216057308)
            self.TT(u.t[:, 0:T], u.t[:, 0:T], gx.t[:, 0:T], ALU.mult, [u.b, gx.b], [u.b])
            P.sput(gx)
            r = P.sget()
            self.ACT(r.t[:, 0:T], br.t[:, :], AF.Sigmoid, [br.b] + rwv, [r.b], bias=self.prm_ap('ba%d' % l, g))
            P.bput(br)
            ii = P.sget()
            self.ACT(ii.t[:, 0:T], bi.t[:, :], AF.Sigmoid, [bi.b] + rwv, [ii.b], bias=self.prm_ap('bx%d' % l, g))
            P.bput(bi)
            self.TT(ii.t[:, 0:T], ii.t[:, 0:T], xr.t[:, 0:T], ALU.mult, [ii.b, xr.b], [ii.b])
            P.sput(xr)
            a2 = P.sget()
            self.ACT(a2.t[:, 0:T], r.t[:, 0:T], AF.Exp, [r.b] + rwv, [a2.b], scale=vec[:, 88 + g:89 + g])
            self.ACT(r.t[:, 0:T], r.t[:, 0:T], AF.Exp, [r.b] + rwv, [r.b], scale=vec[:, 80 + g:81 + g])
            self.ACT(a2.t[:, 0:T], a2.t[:, 0:T], AF.Sqrt, [a2.b], [a2.b], scale=-1.0, bias=1.0)
            self.TT(ii.t[:, 0:T], ii.t[:, 0:T], a2.t[:, 0:T], ALU.mult, [ii.b, a2.b], [ii.b])
            P.sput(a2)
            hb = P.sget()
            P.op('dve', lambda e, o=hb.t[:, 0:T], a=r.t[:, 0:T], b=ii.t[:, 0:T], init=self.hst[l][:, g:g + 1]:
                 e.tensor_tensor_scan(out=o, data0=a, data1=b, initial=init, op0=ALU.mult, op1=ALU.add),
                 [r.b, ii.b, self.hstB[l]], [hb.b])
            P.sput(r)
            P.sput(ii)
            self.CP('act', self.hst[l][:, g:g + 1], hb.t[:, T - 1:T], [hb.b], [self.hstB[l]])
            self.TT(act[:, 24 + g, :], u.t[:, 0:T], hb.t[:, 0:T], ALU.mult, [u.b, hb.b], [actB[24 + g]])
            P.sput(u)
            P.sput(hb)
            if g == 7:
                self.ada_bg(1)
        kvf = []
        kacc = self.acc_begin()
        for c in range(4):
            bk = self.proj16(win, 40 + c)
            s_ = P.sget()
            self.CP('act', s_.t[:, 0:T], bk.t[:, :], [bk.b], [s_.b])
            P.bput(bk)
            kvf.append(s_)
            self.acc_add(kacc, s_.t[:, 0:T], s_.b)
        bA = self.proj16(win, 44)
        self.rope_apply(bA, self.kpe[l][:, t0:t0 + T], self.kpeB[l])
        rstd = self.acc_finish(kacc, 1.0 / 512)
        for c in range(4):
            self.STT(self.ckv[l][:, c, t0:t0 + T], kvf[c].t[:, 0:T], self.prm_ap('kvg%d' % l, c), rstd.t[:, 0:T],
                     ALU.mult, ALU.mult, [kvf[c].b, rstd.b] + rwv, [self.ckvB[l]])
            P.sput(kvf[c])
        P.sput(rstd)
        for hp in range(8):
            for e_ in range(2):
                h = 2 * hp + e_
                bq = self.proj16(win, 45 + 3 * hp + e_)
                self.CP('act', act[:, h, :], bq.t[:, :], [bq.b], [actB[h]])
                P.bput(bq)
            bA = self.proj16(win, 45 + 3 * hp + 2)
            self.rope_apply(bA, act[:, 16 + hp, :], actB[16 + hp])
            if hp in (3, 7):
                self.ada_bg(1)
        nkb = 4 * (t + 1)
        kpe = self.kpe[l]
        uk = P.wnext(W['ukvk'][0, :, :], 2048)
        self.attn_K(l, t, 0, uk)
        uv = None
        for h in range(16):
            G, hh = divmod(h, 4)
            hp, hb_ = divmod(h, 2)
            base = 64 * hb_
            if hh == 0:
                uv = P.wnext(W['ukvv'][G, :, :], 2048)
            self.attn_V(l, t, h, uv)
            if hh == 3:
                P.wfree(uv)
            bo = P.bget()
            dacc = P.sget()

            def emit_S(kb):
                j = kb - 4 * t
                q0 = 0 if j < 0 else j * 128
                bs = P.bget()
                pairs = [(self.kn[:, kb * 128:(kb + 1) * 128], act[:, h, q0:T]),
                         (kpe[base:base + 64, kb * 128:(kb + 1) * 128], act[base:base + 64, 16 + hp, q0:T])]
                self.mmg(bs.t[:, q0:T], pairs, [self.knB[kb // 4], actB[h], self.kpeB[l], actB[16 + hp]], [bs.b])
                pt = P.pget()
                self.ACT(pt.t[:, q0:T], bs.t[:, q0:T], AF.Exp, [bs.b], [pt.b], scale=SCALE)
                P.bput(bs)
                if j >= 0:
                    self.TT(pt.t[:, q0:q0 + 128], pt.t[:, q0:q0 + 128], self.mask[:, :], ALU.mult,
                            [pt.b, self.cstB], [pt.b])
                return (kb, q0, pt)

            def emit_PV(kb, q0, pt):
                self.mmg(bo.t[:, q0:T], [(self.vg[:, kb * 128:(kb + 1) * 128], pt.t[:, q0:T])],
                         [self.vgB[kb // 4], pt.b], [bo.b], start=(kb == first_kb), stop=(kb == last_kb))
                if kb == first_kb:
                    self.CP('dve', dacc.t[:, 0:T], pt.t[:, :], [pt.b], [dacc.b])
                else:
                    self.TT(dacc.t[:, q0:T], dacc.t[:, q0:T], pt.t[:, q0:T], ALU.add, [dacc.b, pt.b], [dacc.b])
                P.pput(pt)

            pend = []
            order = list(range(4 * t, nkb)) + list(range(4 * t))
            first_kb, last_kb = order[0], order[-1]
            for kb in order:
                pend.append(emit_S(kb))
                if len(pend) > 3:
                    emit_PV(*pend.pop(0))
            while len(pend) > 1:
                emit_PV(*pend.pop(0))
            pend = pend[0]
            if h < 15:
                if (h + 1) % 4 == 0:
                    P.wfree(uk)
                    uk = P.wnext(W['ukvk'][(h + 1) // 4, :, :], 2048)
                self.attn_K(l, t, h + 1, uk)
            else:
                P.wfree(uk)
            emit_PV(*pend)
            bd = P.bget()
            self.mmg(bd.t[:, :], [(self.ones32[:, :], dacc.t[:, 0:T])], [dacc.b, self.cstB], [bd.b])
            self.ACT(dacc.t[:, 0:T], bd.t[:, :], AF.Ln, [bd.b], [dacc.b])
            P.bput(bd)
            self.ACT(dacc.t[:, 0:T], dacc.t[:, 0:T], AF.Exp, [dacc.b], [dacc.b], scale=-1.0)
            self.TT(act[:, h, :], bo.t[:, :], dacc.t[:, 0:T], ALU.mult, [bo.b, dacc.b], [actB[h]])
            P.bput(bo)
            P.sput(dacc)
        if self.dbg and self.stop == ('mixD', l):
            return
        def mch(n):
            return 16 + n if n < 8 else 32 + n

        wl = ws = None
        for n in range(16):
            nn = n % 2
            if nn == 0:
                wl = P.wnext(W['lruout'][n // 2, :, :], 2048)
                ws = P.wnext(W['scout'][n // 2, :, :], 2048)
            byl = P.bget()
            pairs = [(wl[0][:, (nn * 8 + kc) * 128:(nn * 8 + kc + 1) * 128], act[:, 24 + kc, :]) for kc in range(8)]
            self.mmg(byl.t[:, :], pairs, [wl[1]] + actB[24:32], [byl.b])
            bys = P.bget()
            pairs = [(ws[0][:, (nn * 8 + kc) * 128:(nn * 8 + kc + 1) * 128], act[:, 32 + kc, :]) for kc in range(8)]
            self.mmg(bys.t[:, :], pairs, [ws[1]] + actB[32:40], [bys.b])
            if nn == 1:
                P.wfree(wl)
                P.wfree(ws)
            bg0 = self.proj16(win, 69 + 3 * n)
            bg1 = self.proj16(win, 69 + 3 * n + 1)
            bg2 = self.proj16(win, 69 + 3 * n + 2)
            wm = P.wnext(W['mlaout'][n, :, :], 2048)
            bym = P.bget()
            pairs = [(wm[0][:, kc * 128:(kc + 1) * 128], act[:, kc, :]) for kc in range(16)]
            self.mmg(bym.t[:, :], pairs, [wm[1]] + actB[0:16], [bym.b])
            P.wfree(wm)
            s0 = P.sget()
            self.ACT(s0.t[:, 0:T], bg0.t[:, :], AF.Sigmoid, [bg0.b], [s0.b])
            P.bput(bg0)
            self.TT(s0.t[:, 0:T], s0.t[:, 0:T], byl.t[:, :], ALU.mult, [s0.b, byl.b], [s0.b])
            P.bput(byl)
            s1 = P.sget()
            self.ACT(s1.t[:, 0:T], bg1.t[:, :], AF.Sigmoid, [bg1.b], [s1.b])
            P.bput(bg1)
            self.TT(s1.t[:, 0:T], s1.t[:, 0:T], bys.t[:, :], ALU.mult, [s1.b, bys.b], [s1.b])
            P.bput(bys)
            self.TT(s0.t[:, 0:T], s0.t[:, 0:T], s1.t[:, 0:T], ALU.add, [s0.b, s1.b], [s0.b])
            P.sput(s1)
            s2 = P.sget()
            self.ACT(s2.t[:, 0:T], bg2.t[:, :], AF.Sigmoid, [bg2.b], [s2.b])
            P.bput(bg2)
            self.TT(s2.t[:, 0:T], s2.t[:, 0:T], bym.t[:, :], ALU.mult, [s2.b, bym.b], [s2.b])
            P.bput(bym)
            self.TT(act[:, mch(n), :], s0.t[:, 0:T], s2.t[:, 0:T], ALU.add, [s0.b, s2.b], [actB[mch(n)]])
            P.sput(s0)
            P.sput(s2)
            if n == 7:
                self.ada_bg(1)
        if self.dbg and self.stop == ('mixE', l):
            return
        self.ada_need(l, 5)
        self.preload_ln()
        Gm = mod[:, 80:96]
        self.xacc_begin()
        for n in range(16):
            w = P.wnext(W['wo'][n, :, :], 2048)
            bk = P.bget()
            pairs = [(w[0][:, kc * 128:(kc + 1) * 128], act[:, mch(kc), :]) for kc in range(16)]
            self.mmg(bk.t[:, :], pairs, [w[1]] + [actB[mch(kc)] for kc in range(16)], [bk.b])
            P.wfree(w)
            self.STT(self.xt[:, n, :], bk.t[:, :], Gm[:, n:n + 1], self.xt[:, n, :], ALU.mult, ALU.add,
                     [bk.b, self.xB[n], mB[5]], [self.xB[n]])
            P.bput(bk)
            self.xacc_add(n)

    def tile(self, t):
        P = self.P
        t0 = t * T
        src = self.xT[:, :].rearrange("(c p) t -> p c t", p=128)
        self.xacc_begin()
        for c in range(16):
            P.dma('sp', self.xt[:, c, :], src[:, c, t0:t0 + T], writes=[self.xB[c]])
            self.xacc_add(c)
        self.need_rope = t
        done = False
        for l in range(self.n_layers):
            self.cur_layer = l
            self.ffn(l, 0)
            if self.stop == ('ffn1', l):
                done = True
                break
            self.mixer(l, t)
            if self.stop is not None and self.stop[1] == l and self.stop[0] in ('mixD', 'mixE', 'mix'):
                done = True
                break
            self.ffn(l, 1)
            if self.stop == ('ffn2', l):
                done = True
                break
        if not done:
            dst = self.outT[:, :].rearrange("(c p) t -> p c t", p=128)
            rstd = self.acc_finish(self.xacc, 1.0 / D)
            for c in range(16):
                self.STT(self.xt[:, c, :], self.xt[:, c, :], self.prm_ap('fg', c), rstd.t[:, 0:T], ALU.mult, ALU.mult,
                         [self.xB[c], rstd.b, self.prmB], [self.xB[c]])
                tok = P.dma('sp', dst[:, c, t0:t0 + T], self.xt[:, c, :], reads=[self.xB[c]])
                if tok is not None:
                    self.out_toks.append(tok)
            P.sput(rstd)
        else:
            P.sput(self.xacc[0])
            dst = self.outT[:, :].rearrange("(c p) t -> p c t", p=128)[:, :, t0:t0 + T]
            tok = P.dma('sp', dst, self.xt[:, :, :], reads=self.xB)
            if tok is not None:
                self.out_toks.append(tok)
        if self.dbg:
            tok = P.dma('sp', self.dbgact[:, :], self.act[:, :, :].rearrange("p c t -> p (c t)"), reads=self.actB)
            if tok is not None:
                self.out_toks.append(tok)

    def build(self):
        P = self.P
        P.dry = True
        self.prologue()
        n0 = len(P.wseq)
        self.tile(0)
        n1 = len(P.wseq)
        assert self.stop is not None or not self.ada_pending
        if self.nt_run > 1:
            self.tile(1)
        seq = P.wseq[:n1] + P.wseq[n1:] * (self.nt_run - 1)
        P.wseq = seq
        P.dry = False
        P.wpump()
        self.prologue()
        for t in range(self.nt_run):
            self.tile(t)
        assert P.wi == len(P.wseq), (P.wi, len(P.wseq))
        P.wait_tokens('sp', self.out_toks)
        P.emit()


def _tile_units(Wm, ncols=128):
    K, N = Wm.shape
    kc = K // 128
    nu = N // ncols
    return np.ascontiguousarray(Wm.reshape(kc, 128, nu, ncols).transpose(2, 1, 0, 3)).reshape(nu, 128, kc * ncols)


def _vec16(v):
    n = v.shape[0] // 128
    return v.reshape(n, 128).T


def _win_cols():
    LX, LG, SB, SC, SX, Q, KV, KP, GT = 0, 1024, 2048, 3072, 4096, 5120, 8192, 8704, 8768
    cols = []
    r = np.arange
    for g in range(8):
        cols += [LX + g * 128 + r(128), LG + g * 128 + r(128)]
    for g in range(8):
        cols += [SC + g * 128 + r(128), SX + g * 128 + r(128), SB + g * 128 + r(128)]
    for c in range(4):
        cols.append(KV + c * 128 + r(128))
    x1, x2 = KP + r(32), KP + 32 + r(32)
    cols.append(np.concatenate([x1, x2, x1, x2]))
    for hp in range(8):
        h0, h1 = 2 * hp, 2 * hp + 1
        cols.append(Q + h0 * 192 + r(128))
        cols.append(Q + h1 * 192 + r(128))
        a1, a2 = Q + h0 * 192 + 128 + r(32), Q + h0 * 192 + 160 + r(32)
        b1, b2 = Q + h1 * 192 + 128 + r(32), Q + h1 * 192 + 160 + r(32)
        cols.append(np.concatenate([a1, a2, b1, b2]))
    for n in range(16):
        for br in range(3):
            cols.append(GT + br * 2048 + n * 128 + r(128))
    cols = np.concatenate(cols)
    assert cols.shape[0] == 117 * 128
    return cols


def prep_shared(inp):
    sh = {}
    cols = _win_cols()
    for l in range(DEPTH):
        sh["ada%d" % l] = _tile_units(inp["ada_w"][l])
        for f in range(2):
            w13 = inp["ffn_w13"][l, f]
            g = _tile_units(w13[:, :DFF])
            u = _tile_units(w13[:, DFF:])
            sh["w13_%d_%d" % (l, f)] = np.ascontiguousarray(np.stack([g, u], axis=1)).reshape(88, 128, 2048)
            sh["w2_%d_%d" % (l, f)] = _tile_units(inp["ffn_w2"][l, f])
        sh["win%d" % l] = _tile_units(inp["w_in"][l][:, cols])
        wa = inp["lru_wa"][l]
        wx = inp["lru_wx"][l]
        sh["lruw%d" % l] = np.ascontiguousarray(np.stack([wa, wx], axis=2)).reshape(8, 128, 256)
        for nm, key in (("lruout", "lru_out"), ("scout", "sc_out")):
            u = _tile_units(inp[key][l])
            sh["%s%d" % (nm, l)] = np.ascontiguousarray(u.reshape(8, 2, 128, 1024).transpose(0, 2, 1, 3)).reshape(8, 128, 2048)
        ukv = inp["mla_w_ukv"][l].reshape(4, 128, 4, 4, 2, 128)
        sh["ukvk%d" % l] = np.ascontiguousarray(ukv[:, :, :, :, 0, :].transpose(2, 1, 3, 0, 4)).reshape(4, 128, 2048)
        sh["ukvv%d" % l] = np.ascontiguousarray(ukv[:, :, :, :, 1, :].transpose(2, 1, 0, 3, 4)).reshape(4, 128, 2048)
        sh["mlaout%d" % l] = _tile_units(inp["mla_out"][l])
        sh["wo%d" % l] = _tile_units(inp["w_o"][l])
    perm = np.zeros((128, 128), np.float32)
    perm[np.arange(128) ^ 32, np.arange(128)] = 1.0
    sh["cst"] = np.ascontiguousarray(np.concatenate([np.triu(np.ones((128, 128), np.float32)), perm], axis=1))
    return sh


def prep_core(inp, b):
    prm = np.zeros((128, NPRM), np.float32)

    def put(name, arr):
        arr = np.asarray(arr, np.float32)
        prm[:, OFF[name]:OFF[name] + arr.shape[1]] = arr

    put('c', _vec16(inp["c"][b]))
    for l in range(DEPTH):
        put('ada_b%d' % l, _vec16(inp["ada_b"][l]))
        put('ng%d' % l, np.concatenate([_vec16(inp["norm_g"][l, i]) for i in range(3)], axis=1))
        put('cw%d' % l, inp["lru_conv_w"][l].reshape(4, 8, 128).transpose(2, 1, 0).reshape(128, 32))
        put('cb%d' % l, _vec16(inp["lru_conv_b"][l]))
        put('ba%d' % l, _vec16(inp["lru_ba"][l]))
        put('bx%d' % l, _vec16(inp["lru_bx"][l]))
        put('lam%d' % l, _vec16(inp["lru_lambda"][l]))
        put('scw%d' % l, inp["sc_conv_w"][l].reshape(3, 8, 128).transpose(2, 1, 0).reshape(128, 24))
        put('kvg%d' % l, _vec16(inp["mla_kv_norm_g"][l]))
    put('fg', _vec16(inp["final_norm_g"]))
    half = 32
    inv = (np.float32(10000.0) ** (-np.arange(half, dtype=np.float32) / np.float32(half))).astype(np.float32)
    put('invf', np.tile(inv, 4)[:, None])
    put('sgn', np.tile(np.concatenate([-np.ones(32, np.float32), np.ones(32, np.float32)]), 2)[:, None])
    m = {"prm": prm,
         "xT": np.ascontiguousarray(inp["x"][b].T),
         "pos": np.ascontiguousarray(np.broadcast_to(inp["positions"][b].astype(np.int32)[None, :], (128, S)))}
    return m


_CACHE = {}


def get_nc(nt_run=NT, n_layers=DEPTH, stop=None, dbg=False):
    key = (nt_run, n_layers, stop, dbg)
    if key not in _CACHE:
        nc = bass.Bass("TRN2", target_bir_lowering=False)
        g = Gen(nc, nt_run, n_layers, stop, dbg)
        g.build()
        _CACHE[key] = nc
    return _CACHE[key]


def kernel(**inputs):
    inp = {k: np.asarray(v) for k, v in inputs.items()}
    B = inp["x"].shape[0]
    sh = prep_shared(inp)
    in_maps = []
    for b in range(B):
        m = prep_core(inp, b)
        m.update(sh)
        in_maps.append(m)
    nc = get_nc()
    res = run_bass_kernel_spmd(nc, in_maps, core_ids=list(range(B)))
    out = np.stack([np.ascontiguousarray(res.results[b]["outT"].T) for b in range(B)], axis=0)
    return out.astype(np.float32)
```
